# Optimizing a Trainium2 kernel written in Bass

```python
import math
import jax, jax.numpy as jnp
from jax import lax
import numpy as np

D_MODEL = 1024
BATCH = 16
SEQ = 256
DEPTH = 4
DEC_BATCH = 2
DEC_SEQ = 1024
PAST_LEN = 512

GRID_W = 64
N_EVEN = (DEPTH + 1) // 2
N_ODD = DEPTH // 2
Q_BLOCK = 128
ROPE_THETA = 10000.0
NORM_EPS = 1e-6
DH = 64
A_W = D_MODEL // 2
CONV_K = 31
H_B = D_MODEL // (4 * DH)
B_W = H_B * 2 * DH
H_C = D_MODEL // (2 * DH)
KV_C = H_C // 4
G_C = H_C // KV_C
C_W = H_C * DH
H_D = D_MODEL // (2 * DH)
KV_D = H_D // 4
G_D = H_D // KV_D
D_W = H_D * DH
WINDOW = 128

IN_E = 3 * A_W + 4 * B_W
OUT_E = A_W + B_W
SPLIT_E = [A_W, 2 * A_W, 3 * A_W, 3 * A_W + B_W, 3 * A_W + 2 * B_W, 3 * A_W + 3 * B_W]
IN_O = 2 * C_W + 2 * KV_C * DH + 2 * D_W + 2 * KV_D * DH
OUT_O = C_W + D_W
_o = np.cumsum([C_W, KV_C * DH, KV_C * DH, C_W, D_W, KV_D * DH, KV_D * DH])
SPLIT_O = [int(v) for v in _o]

kernel_name = "hybrid_diffusion_prefix_step"

F32 = jnp.float32


def rms_norm(x, g):
    xf = x.astype(F32)
    y = xf * lax.rsqrt(jnp.mean(xf * xf, axis=-1, keepdims=True) + NORM_EPS)
    return (y * g.astype(F32)).astype(x.dtype)


def layer_norm(x, g, b):
    xf = x.astype(F32)
    mu = jnp.mean(xf, axis=-1, keepdims=True)
    var = jnp.mean(jnp.square(xf - mu), axis=-1, keepdims=True)
    y = (xf - mu) * lax.rsqrt(var + NORM_EPS) * g.astype(F32) + b.astype(F32)
    return y.astype(x.dtype)


def rope_2d(x):
    T, d = x.shape[1], x.shape[-1]
    rows = T // GRID_W
    nf = d // 4
    row = jnp.repeat(jnp.arange(rows), GRID_W).astype(F32)
    col = jnp.tile(jnp.arange(GRID_W), rows).astype(F32)
    inv = ROPE_THETA ** (-jnp.arange(nf, dtype=F32) / nf)
    ang = jnp.stack([row[:, None] * inv, col[:, None] * inv], axis=1)
    ang = ang.reshape((T,) + (1,) * (x.ndim - 3) + (2, nf))
    cos, sin = jnp.cos(ang), jnp.sin(ang)
    xr = x.astype(F32).reshape(x.shape[:-1] + (2, 2, nf))
    x1, x2 = xr[..., 0, :], xr[..., 1, :]
    out = jnp.stack([x1 * cos - x2 * sin, x2 * cos + x1 * sin], axis=-2)
    return out.reshape(x.shape).astype(x.dtype)


def over_query_blocks(fn, q):
    b, T = q.shape[:2]
    nb = T // Q_BLOCK
    qb = jnp.moveaxis(q.reshape((b, nb, Q_BLOCK) + q.shape[2:]), 1, 0)
    ob = lax.map(fn, qb)
    return jnp.moveaxis(ob, 0, 1).reshape((b, T) + ob.shape[3:])


def diff_attend(q, k, v, lam):
    s = jnp.einsum('bqhcd,bkhcd->bhcqk', q, k).astype(F32) * (q.shape[-1] ** -0.5)
    p = jax.nn.softmax(s, axis=-1)
    w = p[:, :, 0] - lam * p[:, :, 1]
    return jnp.einsum('bhqk,bkhv->bqhv', w.astype(v.dtype), v)


def gqa_attend(q, k, v, sink=None):
    s = jnp.einsum('bqngd,bknd->bngqk', q, k).astype(F32) * (q.shape[-1] ** -0.5)
    if sink is None:
        p = jax.nn.softmax(s, axis=-1)
    else:
        sk = jnp.broadcast_to(sink.astype(F32)[None, :, :, None, None], s.shape[:-1] + (1,))
        p = jax.nn.softmax(jnp.concatenate([s, sk], axis=-1), axis=-1)[..., :-1]
    return jnp.einsum('bngqk,bknd->bqngd', p.astype(v.dtype), v)


def banded_sink_attend(q, k, v, ck, cv, sink):
    b, T, N, G, d = q.shape
    nb = T // Q_BLOCK
    L = ck.shape[1]
    scale = d ** -0.5

    def band(a):
        ab = a.reshape(b, nb, Q_BLOCK, N, d)
        ap = jnp.pad(ab, ((0, 0), (1, 1), (0, 0), (0, 0), (0, 0)))
        return jnp.concatenate([ap[:, :-2], ap[:, 1:-1], ap[:, 2:]], axis=2)

    qb = jnp.moveaxis(q.reshape(b, nb, Q_BLOCK, N, G, d), 1, 0)
    kb = jnp.moveaxis(band(k), 1, 0)
    vb = jnp.moveaxis(band(v), 1, 0)
    blk = jnp.arange(nb)[:, None, None]
    qpos = blk * Q_BLOCK + jnp.arange(Q_BLOCK)[None, :, None]
    kpos = (blk - 1) * Q_BLOCK + jnp.arange(3 * Q_BLOCK)[None, None, :]
    valid = (jnp.abs(qpos - kpos) <= WINDOW) & (kpos >= 0) & (kpos < T)

    def one(args):
        qi, ki, vi, mi = args
        s_loc = jnp.einsum('bqngd,bknd->bngqk', qi, ki).astype(F32) * scale
        s_loc = jnp.where(mi[None, None, None], s_loc, -jnp.inf)
        s_ctx = jnp.einsum('bqngd,bknd->bngqk', qi, ck).astype(F32) * scale
        sk = jnp.broadcast_to(sink.astype(F32)[None, :, :, None, None], s_ctx.shape[:-1] + (1,))
        p = jax.nn.softmax(jnp.concatenate([s_ctx, s_loc, sk], axis=-1), axis=-1)
        p_ctx = p[..., :L].astype(v.dtype)
        p_loc = p[..., L:L + 3 * Q_BLOCK].astype(v.dtype)
        return (jnp.einsum('bngqk,bknd->bqngd', p_ctx, cv)
                + jnp.einsum('bngqk,bknd->bqngd', p_loc, vi))

    ob = lax.map(one, (qb, kb, vb, valid))
    return jnp.moveaxis(ob, 0, 1).reshape(b, T, N, G, d)


def even_mix(h, w_in, conv_w, conv_b, ln_g, ln_b, lam_vec, subln_g, w_out, lam_init, ctx_kv):
    b, T, _ = h.shape
    a_u, a_g, a_z, b_q, b_k, b_v, b_z = jnp.split(h @ w_in, SPLIT_E, axis=-1)
    a = a_u * jax.nn.sigmoid(a_g)
    a = lax.conv_general_dilated(a, conv_w[:, None, :].astype(a.dtype), window_strides=(1,),
                                 padding=[(CONV_K // 2, CONV_K // 2)],
                                 dimension_numbers=('NWC', 'WIO', 'NWC'),
                                 feature_group_count=A_W) + conv_b
    a = jax.nn.silu(layer_norm(a, ln_g, ln_b)) * jax.nn.silu(a_z)
    q = b_q.reshape(b, T, H_B, 2, DH)
    k = b_k.reshape(b, T, H_B, 2, DH)
    v = b_v.reshape(b, T, H_B, 2 * DH)
    lv = lam_vec.astype(F32)
    lam = jnp.exp(jnp.sum(lv[0] * lv[1])) - jnp.exp(jnp.sum(lv[2] * lv[3])) + lam_init
    if ctx_kv is None:
        keys, vals, new_kv = k, v, (k, v)
    else:
        q, k = rope_2d(q), rope_2d(k)
        keys = jnp.concatenate([ctx_kv[0], k], axis=1)
        vals = jnp.concatenate([ctx_kv[1], v], axis=1)
        new_kv = None
    o = over_query_blocks(lambda qb: diff_attend(qb, keys, vals, lam), q)
    o = rms_norm(o, subln_g) * (1.0 - lam_init)
    o = o.reshape(b, T, B_W) * jax.nn.silu(b_z)
    return jnp.concatenate([a, o], axis=-1) @ w_out, new_kv


def odd_mix(h, w_in, q_norm, k_norm, sink, w_out, ctx_kv):
    b, T, _ = h.shape
    c_q, c_k, c_v, c_z, d_q, d_k, d_v, d_z = jnp.split(h @ w_in, SPLIT_O, axis=-1)
    cq = rms_norm(c_q.reshape(b, T, KV_C, G_C, DH), q_norm)
    ck = rms_norm(c_k.reshape(b, T, KV_C, DH), k_norm)
    cv = c_v.reshape(b, T, KV_C, DH)
    dq = d_q.reshape(b, T, KV_D, G_D, DH)
    dk = d_k.reshape(b, T, KV_D, DH)
    dv = d_v.reshape(b, T, KV_D, DH)
    sk = sink.reshape(KV_D, G_D)
    if ctx_kv is None:
        oc = over_query_blocks(lambda qb: gqa_attend(qb, ck, cv), cq)
        od = over_query_blocks(lambda qb: gqa_attend(qb, dk, dv, sk), dq)
        new_kv = (ck, cv, dk, dv)
    else:
        cck, ccv, cdk, cdv = ctx_kv
        cq, ck, dq, dk = rope_2d(cq), rope_2d(ck), rope_2d(dq), rope_2d(dk)
        keys = jnp.concatenate([cck, ck], axis=1)
        vals = jnp.concatenate([ccv, cv], axis=1)
        oc = over_query_blocks(lambda qb: gqa_attend(qb, keys, vals), cq)
        od = banded_sink_attend(dq, dk, dv, cdk, cdv, sk)
        new_kv = None
    oc = oc.reshape(b, T, C_W) * jax.nn.silu(c_z)
    od = od.reshape(b, T, D_W) * jax.nn.silu(d_z)
    return jnp.concatenate([oc, od], axis=-1) @ w_out, new_kv


def modulation(cond, w_mod, b_mod):
    m = (jax.nn.silu(cond) @ w_mod + b_mod).reshape(-1, 1, 3 * D_MODEL)
    return jnp.split(m, 3, axis=-1)


def setup_inputs(seed: int = 0) -> dict:
    key = jax.random.key(seed)
    ks = iter(jax.random.split(key, 32))
    nrm = lambda shape, s=1.0: jax.random.normal(next(ks), shape, F32) * s
    return {
        'x_prompt': nrm((BATCH, SEQ, D_MODEL)),
        'x_sample': nrm((DEC_BATCH, DEC_SEQ, D_MODEL)),
        'cache_b_k': nrm((DEC_BATCH, N_EVEN, PAST_LEN, H_B, 2, DH)),
        'cache_b_v': nrm((DEC_BATCH, N_EVEN, PAST_LEN, H_B, 2 * DH)),
        'cache_c_k': nrm((DEC_BATCH, N_ODD, PAST_LEN, KV_C, DH)),
        'cache_c_v': nrm((DEC_BATCH, N_ODD, PAST_LEN, KV_C, DH)),
        'cache_d_k': nrm((DEC_BATCH, N_ODD, PAST_LEN, KV_D, DH)),
        'cache_d_v': nrm((DEC_BATCH, N_ODD, PAST_LEN, KV_D, DH)),
        'c': nrm((DEC_BATCH, D_MODEL)),
        'c_ctx': nrm((D_MODEL,)),
        'norm_pre': 1.0 + nrm((DEPTH, D_MODEL), 0.1),
        'norm_post': 1.0 + nrm((DEPTH, D_MODEL), 0.1),
        'w_mod': nrm((DEPTH, D_MODEL, 3 * D_MODEL), 0.5 * D_MODEL ** -0.5),
        'b_mod': nrm((DEPTH, 3 * D_MODEL), 0.02),
        'w_in_even': nrm((N_EVEN, D_MODEL, IN_E), D_MODEL ** -0.5),
        'a_conv_w': nrm((N_EVEN, CONV_K, A_W), CONV_K ** -0.5),
        'a_conv_b': nrm((N_EVEN, A_W), 0.02),
        'a_ln_g': 1.0 + nrm((N_EVEN, A_W), 0.1),
        'a_ln_b': nrm((N_EVEN, A_W), 0.02),
        'b_lambda': nrm((N_EVEN, 4, DH), 0.1),
        'b_subln_g': 1.0 + nrm((N_EVEN, 2 * DH), 0.1),
        'w_out_even': nrm((N_EVEN, OUT_E, D_MODEL), OUT_E ** -0.5),
        'w_in_odd': nrm((N_ODD, D_MODEL, IN_O), D_MODEL ** -0.5),
        'c_q_norm': 1.0 + nrm((N_ODD, DH), 0.1),
        'c_k_norm': 1.0 + nrm((N_ODD, DH), 0.1),
        'd_sink': nrm((N_ODD, H_D), 0.5),
        'w_out_odd': nrm((N_ODD, OUT_O, D_MODEL), OUT_O ** -0.5),
    }


def reference(x_prompt, x_sample, cache_b_k, cache_b_v, cache_c_k, cache_c_v, cache_d_k, cache_d_v,
              c, c_ctx, norm_pre, norm_post, w_mod, b_mod, w_in_even, a_conv_w, a_conv_b, a_ln_g,
              a_ln_b, b_lambda, b_subln_g, w_out_even, w_in_odd, c_q_norm, c_k_norm, d_sink,
              w_out_odd):
    xp, xs = x_prompt, x_sample
    nbk, nbv, nck, ncv, ndk, ndv = [], [], [], [], [], []
    for l in range(DEPTH):
        i = l // 2
        sh_p, sc_p, g_p = modulation(c_ctx, w_mod[l], b_mod[l])
        sh_s, sc_s, g_s = modulation(c, w_mod[l], b_mod[l])
        hp = rms_norm(xp, norm_pre[l]) * (1.0 + sc_p) + sh_p
        hs = rms_norm(xs, norm_pre[l]) * (1.0 + sc_s) + sh_s
        if l % 2 == 0:
            lam_init = 0.8 - 0.6 * math.exp(-0.3 * l)
            args = (w_in_even[i], a_conv_w[i], a_conv_b[i], a_ln_g[i], a_ln_b[i],
                    b_lambda[i], b_subln_g[i], w_out_even[i], lam_init)
            op, (kp, vp) = even_mix(hp, *args, None)
            os_, _ = even_mix(hs, *args, (cache_b_k[:, i], cache_b_v[:, i]))
            nbk.append(kp)
            nbv.append(vp)
        else:
            args = (w_in_odd[i], c_q_norm[i], c_k_norm[i], d_sink[i], w_out_odd[i])
            op, (ck, cv, dk, dv) = odd_mix(hp, *args, None)
            os_, _ = odd_mix(hs, *args, (cache_c_k[:, i], cache_c_v[:, i],
                                         cache_d_k[:, i], cache_d_v[:, i]))
            nck.append(ck)
            ncv.append(cv)
            ndk.append(dk)
            ndv.append(dv)
        xp = xp + g_p * rms_norm(op, norm_post[l])
        xs = xs + g_s * rms_norm(os_, norm_post[l])
    return (xp, xs, jnp.stack(nbk, axis=1), jnp.stack(nbv, axis=1), jnp.stack(nck, axis=1),
            jnp.stack(ncv, axis=1), jnp.stack(ndk, axis=1), jnp.stack(ndv, axis=1))
```

```python
import math
from contextlib import ExitStack

import numpy as np
import concourse.bass as bass
import concourse.mybir as mybir
from concourse.bass_utils import run_bass_kernel_spmd

F32 = mybir.dt.float32
BF16 = mybir.dt.bfloat16
AF = mybir.ActivationFunctionType
ALU = mybir.AluOpType

D_MODEL = 1024
DEPTH = 4
SEQ = 256
DEC_SEQ = 1024
PAST = 512
DH = 64
CONV_K = 31
EPS = 1e-6
N_CORES = 8
DBG = {"layers": 4, "passes": ("P", "S")}

P_NPRE, P_NPOST, P_BMOD, P_CONVW, P_CONVB, P_LNG, P_LNB, P_SUBG, P_LAM, P_QN, P_KN, P_SINK, P_KNF = (
    0, 8, 16, 40, 164, 168, 172, 176, 177, 433, 434, 435, 443)
NPL = 443 + 64
C_ID, C_COS, C_SIN = 0, 128, 1152
NCONST = 2176


PHASE = ["setup"]
LAST = {}


class Op:
    __slots__ = ("eng", "fn", "dma", "deps", "signal", "sigval", "slot", "slotval", "idx", "prewait", "phase")

    def __init__(self, eng, fn, dma):
        self.eng, self.fn, self.dma = eng, fn, dma
        self.deps = []
        self.signal = False
        self.sigval = 0
        self.slot = None
        self.slotval = 0
        self.idx = 0
        self.prewait = None


class Sched:
    COMPUTE = ("pe", "act", "dve")
    QUEUES = {"sp": 24, "pool": 12}

    def __init__(self):
        self.eng_ops = {e: [] for e in ("pe", "act", "dve", "sp", "pool")}
        self.last_w = {}
        self.readers = {}
        self.waited = {}
        self.dma_count = {q: 0 for q in self.QUEUES}
        self.slot_last = {}

    def add(self, eng, fn, R=(), W=(), dma=False):
        op = Op(eng, fn, dma)
        op.phase = PHASE[0]
        deps = []
        for b in R:
            w = self.last_w.get(b)
            if w is not None:
                deps.append(w)
            if isinstance(b, tuple) and b[0] == "ps":
                deps.extend(r for r in self.readers.get(b, ()) if r.eng != eng)
        for b in W:
            w = self.last_w.get(b)
            if w is not None:
                deps.append(w)
            deps.extend(self.readers.get(b, ()))
        lst = self.eng_ops[eng]
        op.idx = len(lst)
        seen = set()
        for d in deps:
            if id(d) in seen:
                continue
            seen.add(id(d))
            if d.dma:
                key = (eng, "dma", id(d))
                if key in self.waited:
                    continue
                self.waited[key] = True
                op.deps.append(d)
            else:
                if d.eng == eng and eng == "pe":
                    continue
                key = (eng, d.eng)
                if self.waited.get(key, -1) >= d.idx:
                    continue
                self.waited[key] = d.idx
                d.signal = True
                op.deps.append(d)
        if dma:
            n = self.dma_count[eng]
            ns = self.QUEUES[eng]
            self.dma_count[eng] = n + 1
            op.slot = (eng, n % ns)
            op.slotval = 16 * (n // ns + 1)
            prev = self.slot_last.get(op.slot)
            if prev is not None:
                op.prewait = prev
            self.slot_last[op.slot] = op
        for b in R:
            self.readers.setdefault(b, []).append(op)
        for b in W:
            self.last_w[b] = op
            self.readers[b] = []
        lst.append(op)
        return op

    def finalize(self):
        for e in self.COMPUTE:
            n = 0
            for op in self.eng_ops[e]:
                if op.signal:
                    n += 1
                    op.sigval = n

    def emit(self, eng_name, eng, sems, final_wait=False):
        for op in self.eng_ops[eng_name]:
            if op.prewait is not None:
                eng.wait_ge(sems[op.prewait.slot], op.prewait.slotval)
            need = {}
            for d in op.deps:
                if d.dma:
                    k, v = d.slot, d.slotval
                else:
                    k, v = d.eng, d.sigval
                if need.get(k, -1) < v:
                    need[k] = v
            for k, v in need.items():
                eng.wait_ge(sems[k], v)
            ins = op.fn(eng)
            if op.dma:
                ins.then_inc(sems[op.slot], 16)
            elif op.signal:
                ins.then_inc(sems[op.eng], 1)
        if final_wait:
            for slot, op in self.slot_last.items():
                if slot[0] == eng_name:
                    eng.wait_ge(sems[slot], op.slotval)


class Pool:
    def __init__(self, name, aps):
        self.name, self.aps, self.i = name, aps, 0

    def get(self):
        i = self.i % len(self.aps)
        self.i += 1
        return self.aps[i], (self.name, i)


def build_program():
    nc = bass.Bass("TRN2", target_bir_lowering=False)
    S = Sched()
    es = ExitStack()

    def dram_in(name, shape):
        return nc.dram_tensor(name, list(shape), F32, kind="ExternalInput").ap()

    def dram_out(name, shape):
        return nc.dram_tensor(name, list(shape), F32, kind="ExternalOutput").ap()

    xp_d = dram_in("xp", (512, 1024))
    xs_d = dram_in("xs", (1024, 1024))
    cond_d = dram_in("cond", (128, 16))
    params_d = dram_in("params", (128, 4 * NPL))
    consts_d = dram_in("consts", (128, NCONST))
    cmat_d = dram_in("cmat", (128, 4, 128))
    sel_d = dram_in("sel", (128, 4))
    dmask_d = dram_in("dmask", (128, 2048))
    ropeown_d = dram_in("ropeown", (128, 512))
    NG = {0: 15, 1: 14, 2: 15, 3: 14}
    wl_d = [dram_in(f"wl{l}", (NG[l], 128, 8, 512)) for l in range(DBG["layers"])]
    cbk_d = dram_in("cbk", (2, 512, 512))
    cbv_d = dram_in("cbv", (2, 512, 512))
    cck_d = dram_in("cck", (2, 512, 128))
    ccv_d = dram_in("ccv", (2, 512, 128))
    cdk_d = dram_in("cdk", (2, 512, 128))
    cdv_d = dram_in("cdv", (2, 512, 128))

    yp_d = dram_out("yp", (512, 1024))
    ys_d = dram_out("ys", (256, 1024))
    nbk_d = dram_out("nbk", (2, 2, 256, 512))
    nbv_d = dram_out("nbv", (2, 2, 256, 512))
    nck_d = dram_out("nck", (2, 2, 256, 128))
    ncv_d = dram_out("ncv", (2, 2, 256, 128))
    ndk_d = dram_out("ndk", (2, 2, 256, 128))
    ndv_d = dram_out("ndv", (2, 2, 256, 128))

    def sb(name, shape, dt):
        return es.enter_context(nc.sbuf_tensor("sb_" + name, list(shape), dt))

    TM = 1024
    xT = sb("xT", (128, 8, TM), F32)
    hT = sb("hT", (128, 8, TM), BF16)
    wring = sb("wring", (128, 3, 8, 512), BF16)
    Z1 = sb("Z1", (128, 4, TM), BF16)
    Z2 = sb("Z2", (128, 4, TM), BF16)
    Q1 = sb("Q1", (128, 4, TM), BF16)
    Q2 = sb("Q2", (128, 4, TM + 32), BF16)
    KB = sb("KB", (128, 4, 1536), BF16)
    VB = sb("VB", (128, 12, 640), BF16)
    consts = sb("consts", (128, NCONST), F32)
    params = sb("params", (128, 4 * NPL), F32)
    cb16 = sb("cb16", (128, 5, 128), BF16)
    ones16 = sb("ones16", (128, 128), BF16)
    ones32 = sb("ones32", (128, 128), F32)
    modv = sb("modv", (128, 4, 24, 2), F32)
    A1 = sb("A1", (128, 4, 8, 2), F32)
    G1 = sb("G1", (128, 4, 8, 2), F32)
    scond = sb("scond", (128, 16), BF16)
    condf = sb("condf", (128, 16), F32)
    lamt = sb("lamt", (128, 8), F32)
    esink = sb("esink", (128, 8), F32)
    cvbuf = sb("cvbuf", (128, 4, 1024), F32)
    diagb = sb("diagb", (128, 8, 128), BF16)
    stage = sb("stage", (128, 2, 1024), F32)
    ostage = sb("ostage", (128, 2, 512), F32)
    kstage = stage[:].rearrange("p a (b f) -> p (a b) f", b=2)
    ybuf = cvbuf[:].rearrange("p c (b f) -> p (c b) f", b=2)
    dmtab = stage[:, 0, :].bitcast(BF16).rearrange("p (k q) -> p k q", k=8)
    DMK = [("stage", 0, 0), ("stage", 0, 1)]
    f32t = sb("f32t", (128, 6, 512), F32)
    b16t = sb("b16t", (128, 6, 512), BF16)
    small = sb("small", (128, 8, 4), F32)
    selt = sb("selt", (128, 4), F32)
    lnneg = sb("lnneg", (128, 8), F32)
    rst = sb("rst", (128, 2, 512), F32)
    dlb = sb("dlb", (128, 1, 512), F32)
    sqb = sb("sqb", (128, 1, 512), BF16)

    ps = es.enter_context(nc.psum_tensor("ps", [128, 8, 512], F32))

    sems = {}
    for e in ("pe", "act", "dve"):
        sems[e] = es.enter_context(nc.semaphore("s_" + e))
    for q, n in Sched.QUEUES.items():
        for i in range(n):
            sems[(q, i)] = es.enter_context(nc.semaphore(f"d_{q}{i}"))

    fpool = Pool("f32t", [f32t[:, i, :] for i in range(6)])
    bpool = Pool("b16t", [b16t[:, i, :] for i in range(6)])
    rpool = Pool("rst", [rst[:, i, :] for i in range(2)])
    dlpool = Pool("dlb", [dlb[:, i, :] for i in range(1)])
    sqpool = Pool("sqb", [sqb[:, i, :] for i in range(1)])
    spool = Pool("small", [small[:, i, :] for i in range(8)])
    opool = Pool("ostage", [ostage[:, i, :] for i in range(2)])
    dpool = Pool("diag", [diagb[:, i, :] for i in range(8)])
    ring_i = [0]
    SK4 = [("stage", a, b) for a in range(2) for b in range(2)]

    def ring():
        b = ring_i[0] % 4
        ring_i[0] += 1
        return b

    def aux2():
        return ring()

    aux_i = [0]

    def aux():
        b = 4 + aux_i[0] % 4
        aux_i[0] += 1
        return b

    def mm(out, lhsT, rhs, start, stop, R, W):
        S.add("pe", lambda e: e.matmul(out, lhsT=lhsT, rhs=rhs, start=start, stop=stop), R=R, W=W)

    def tr(out, in_, R, W):
        S.add("pe", lambda e: e.transpose(out, in_, consts[:, C_ID:C_ID + 128]), R=R, W=W)

    def act(out, in_, func, R, W, scale=None, bias=None, accum=None):
        kw = {}
        if scale is not None:
            kw["scale"] = scale
        if bias is not None:
            kw["bias"] = bias
        if accum is not None:
            kw["accum_out"] = accum
        S.add("act", lambda e: e.activation(out=out, in_=in_, func=func, **kw), R=R, W=W)

    def tt(out, in0, in1, op, R, W):
        S.add("dve", lambda e: e.tensor_tensor(out=out, in0=in0, in1=in1, op=op), R=R, W=W)

    def stt(out, in0, scalar, in1, op0, op1, R, W):
        S.add("dve", lambda e: e.scalar_tensor_tensor(out=out, in0=in0, scalar=scalar, in1=in1, op0=op0, op1=op1),
              R=R, W=W)

    def ts(out, in0, s1, s2, op0, op1, R, W):
        if op1 is None:
            S.add("dve", lambda e: e.tensor_scalar(out=out, in0=in0, scalar1=s1, scalar2=None, op0=op0), R=R, W=W)
        else:
            S.add("dve", lambda e: e.tensor_scalar(out=out, in0=in0, scalar1=s1, scalar2=s2, op0=op0, op1=op1),
                  R=R, W=W)

    def vcopy(out, in_, R, W):
        S.add("dve", lambda e: e.tensor_copy(out=out, in_=in_), R=R, W=W)

    def vmemset(ap, val, W):
        S.add("dve", lambda e: e.memset(ap, val), R=(), W=W)

    def dma(q, out, in_, R, W):
        S.add(q, lambda e: e.dma_start(out=out, in_=in_), R=R, W=W, dma=True)

    cp_i = [0]

    def evac_copy(out, in_, R, W, eng=None):
        cp_i[0] += 1
        if eng == "act" or (eng is None and cp_i[0] % 2):
            act(out, in_, AF.Copy, R, W)
        else:
            vcopy(out, in_, R, W)

    def rstd_from(ps_ap, scale, R, W_out_ap, Wkey, n=512, parts=slice(0, 128)):
        tmp, tk = fpool.get()
        act(tmp[parts, :n], ps_ap, AF.Ln, R, [tk], scale=scale, bias=epsb[parts, 0:1])
        act(W_out_ap, tmp[parts, :n], AF.Exp, [tk], [Wkey], scale=-0.5)

    def run_stream(stream, L=1, extra=()):
        assert len(stream) % 2 == 0
        pairs = [(stream[i], stream[i + 1]) for i in range(0, len(stream), 2)]
        blocks = []
        cur, tot = [], 0
        for p in pairs:
            w = max(p[0]["w"], p[1]["w"])
            if cur and tot + w > 512:
                blocks.append(cur)
                cur, tot = [], 0
            cur.append(p)
            tot += w
        if cur:
            blocks.append(cur)
        state = {}
        pending = [[d, fn] for d, fn in extra]
        for bi in range(len(blocks) + L):
            if bi < len(blocks):
                if ring_i[0] % 2:
                    ring()
                banks = (ring(), ring())
                offs = []
                c0 = 0
                for p in blocks[bi]:
                    for lane in range(2):
                        p[lane]["mm"](banks[lane], c0)
                    offs.append(c0)
                    c0 += max(p[0]["w"], p[1]["w"])
                if bpool.i % 2:
                    bpool.get()
                i0 = bpool.i % len(bpool.aps)
                tiles = [bpool.get(), bpool.get()]
                act(b16t[:, i0:i0 + 2, :c0], ps[:, banks[0]:banks[0] + 2, :c0], AF.Exp,
                    [("ps", banks[0]), ("ps", banks[1])], [tiles[0][1], tiles[1][1]], scale=0.125)
                for p, o in zip(blocks[bi], offs):
                    for lane in range(2):
                        if p[lane]["post"] is not None:
                            p[lane]["post"](tiles[lane][0], tiles[lane][1], o)
                state[bi] = (tiles, offs)
            if bi >= L:
                tiles, offs = state.pop(bi - L)
                for p, o in zip(blocks[bi - L], offs):
                    for lane in range(2):
                        p[lane]["pv"](tiles[lane][0], tiles[lane][1], o)
                        for d, fn in p[lane]["fins"]:
                            pending.append([d, fn])
            keep = []
            for it in pending:
                if it[0] <= 0:
                    it[1]()
                else:
                    it[0] -= 1
                    keep.append(it)
            pending[:] = keep
        for it in pending:
            it[1]()

    wq = []
    wstate = {"n": 0, "issued": 0}

    def w_issue_upto(n):
        while wstate["issued"] < min(n, len(wq)):
            i = wstate["issued"]
            l, g = wq[i]
            slot = i % 3
            dma("pool", wring[:, slot], wl_d[l][g], R=(), W=[("w", slot)])
            wstate["issued"] += 1

    def w_next(issue=True):
        i = wstate["n"]
        wstate["n"] += 1
        if issue:
            w_issue_upto(i + 3)
        return i % 3

    epsb = sb("epsb", (128, 2), F32)
    dma("sp", consts[:], consts_d, R=(), W=["consts"])
    dma("sp", params[:], params_d, R=(), W=["params"])
    dma("sp", condf[:], cond_d, R=(), W=["condf"])
    dma("sp", selt[:], sel_d, R=(), W=["sel"])
    vmemset(epsb[:, 0:1], EPS, ["epsb"])
    vmemset(epsb[:, 1:2], 1.0, ["epsb"])
    vmemset(ones16[:], 1.0, ["ones16"])
    vmemset(ones32[:], 1.0 / 512.0, ["ones32"])
    dma("pool", cb16[:, 0:4, :], cmat_d, R=(), W=["cb16m"])
    vcopy(cb16[:, 4, :], consts[:, C_ID:C_ID + 128], ["consts"], ["cb16"])
    Rm, BD, MN, MP = cb16[:, 0, :], cb16[:, 1, :], cb16[:, 2, :], cb16[:, 3, :]
    act(scond[:], condf[:], AF.Silu, ["condf"], ["scond"])
    cosT = consts[:, C_COS:C_COS + 1024]
    sinT = consts[:, C_SIN:C_SIN + 1024]

    def pcol(l, off, n=1):
        return params[:, l * NPL + off: l * NPL + off + n]

    for pi_, g_ in enumerate(DBG["passes"]):
        for l in range(DBG["layers"]):
            if pi_ == 0 and l == 0:
                wq.extend((0, g) for g in range(6))
            if g_ == "S" and l == 3 and DBG["layers"] == 4:
                wq.extend((l, g) for g in (10, 11, 6, 7, 8, 9))
            else:
                wq.extend((l, g) for g in range(6, NG[l] - 2))
            if pi_ == 0 and l + 1 < DBG["layers"]:
                wq.extend((l + 1, g) for g in range(6))
            wq.extend((l, g) for g in range(NG[l] - 2, NG[l]))

    def run_pass(grp):
        isS = grp == "S"
        T = 1024 if isS else 512
        NT = T // 512
        cond_i = 1 if isS else 0
        x_d = xs_d if isS else xp_d
        y_d = ys_d if isS else yp_d
        if isS:
            units = [dict(q0=0, qn=1024, kcols=[i * 128 for i in range(12)], vblk=list(range(12)), seq=0)]
            knew0, vnew0 = 512, 4
        else:
            units = [dict(q0=u * 256, qn=256, kcols=[u * 256, u * 256 + 128], vblk=[u * 2, u * 2 + 1], seq=u)
                     for u in range(2)]
            knew0, vnew0 = 0, 0
        apad_base = (lambda col: 15 + col) if isS else (lambda col: 15 + col + 30 * (col // 256))

        def xk(c, j):
            return ("x", c, j)

        def hk(c, j):
            return ("h", c, j)

        def emit_mod(l):
            PHASE[0] = f"{grp}{l}:mod"
            MB_dummy = None
            MB = 7
            for g in range(6):
                slot = w_next()
                for n in range(4):
                    ch = g * 4 + n
                    for kc in range(8):
                        mm(ps[:, MB, ch * 2:ch * 2 + 2], wring[:, slot, kc, n * 128:(n + 1) * 128],
                           scond[:, kc * 2:kc * 2 + 2], kc == 0, kc == 7,
                           R=[("w", slot), "scond"], W=[("ps", MB)])
            for k in range(2):
                tt(modv[:, l, :, k], ps[:, MB, 0:48].rearrange("p (c k) -> p c k", k=2)[:, :, k],
                   pcol(l, P_BMOD, 24), ALU.add, R=[("ps", MB), "params"], W=[("modv", l, k)])
                stt(A1[:, l, :, k], modv[:, l, 8:16, k], 1.0, pcol(l, P_NPRE, 8), ALU.add, ALU.mult,
                    R=[("modv", l, k), "params"], W=[("A1", l, k)])
                tt(G1[:, l, :, k], modv[:, l, 16:24, k], pcol(l, P_NPOST, 8), ALU.mult,
                   R=[("modv", l, k), "params"], W=[("G1", l, k)])

        def mod_closures(ln):
            out = []

            def modg(g):
                slot = w_next()
                b = ring()
                for n in range(4):
                    for kc in range(8):
                        mm(ps[:, b, n * 2:n * 2 + 2], wring[:, slot, kc, n * 128:(n + 1) * 128],
                           scond[:, kc * 2:kc * 2 + 2], kc == 0, kc == 7, R=[("w", slot), "scond"], W=[("ps", b)])
                for k in range(2):
                    tt(modv[:, ln, g * 4:(g + 1) * 4, k], ps[:, b, 0:8].rearrange("p (c k) -> p c k", k=2)[:, :, k],
                       pcol(ln, P_BMOD + g * 4, 4), ALU.add, R=[("ps", b), "params"], W=[("modvp", ln, g, k)])

            def modfin():
                for k in range(2):
                    allp = [("modvp", ln, g, k) for g in range(6)]
                    stt(A1[:, ln, :, k], modv[:, ln, 8:16, k], 1.0, pcol(ln, P_NPRE, 8), ALU.add, ALU.mult,
                        R=allp + ["params"], W=[("A1", ln, k), ("modv", ln, k)])
                    tt(G1[:, ln, :, k], modv[:, ln, 16:24, k], pcol(ln, P_NPOST, 8), ALU.mult,
                       R=allp + ["params"], W=[("G1", ln, k)])

            for g in range(6):
                out.append((1 + 2 * g, (lambda g=g: modg(g))))
            out.append((12, modfin))
            return out

        first_mod_done = [False]
        PHASE[0] = f"{grp}:prologue"
        def prologue_blocks(xd, blks):
            for blk in blks:
                st = blk % 2
                dma("sp", stage[:, st, :], xd[blk * 128:(blk + 1) * 128, :], R=(), W=[("stage", st, 0), ("stage", st, 1)])
                for half in range(2):
                    b = ring()
                    for q in range(4):
                        fc = half * 4 + q
                        tr(ps[:, b, q * 128:(q + 1) * 128], stage[:, st, fc * 128:(fc + 1) * 128],
                           R=[("stage", st, 0), ("stage", st, 1), "consts"], W=[("ps", b)])
                    evac_copy(xT[:, half * 4:half * 4 + 4, blk * 128:(blk + 1) * 128],
                              ps[:, b, :].rearrange("p (q t) -> p q t", q=4),
                              R=[("ps", b)], W=[xk(half * 4 + q, blk // 4) for q in range(4)], eng="act")

        hoist = DBG["passes"] == ("P", "S")
        nblk0 = (4 if hoist else 8) if isS else 4
        if grp == DBG["passes"][0] and DBG["layers"] > 0:
            cl0 = mod_closures(0)
            for i_, (_d, fn_) in enumerate(cl0[:6]):
                fn_()
                if i_ < nblk0:
                    prologue_blocks(x_d, [i_])
            prologue_blocks(x_d, range(6, nblk0))
            cl0[6][1]()
        else:
            prologue_blocks(x_d, range(nblk0))

        for l in range(DBG["layers"]):
            even = l % 2 == 0
            li = l // 2
            last = isS and l == 3 and DBG["layers"] == 4
            next_mod = mod_closures(l + 1) if (grp == DBG["passes"][0] and l + 1 < DBG["layers"]) else []
            lam_init = 0.8 - 0.6 * math.exp(-0.3 * l)

            if DBG.get("stop") == "mod":
                return
            ck = cond_i
            modR = [("modv", l, ck), ("A1", l, ck), ("G1", l, ck)]

            PHASE[0] = f"{grp}{l}:prenorm"
            SB_ = 6
            for j in range(NT):
                cs = slice(j * 512, (j + 1) * 512)
                for c in range(8):
                    sq, sk = bpool.get()
                    act(sq, xT[:, c, cs], AF.Square, [xk(c, j)], [sk])
                    mm(ps[:, SB_, :], ones16[:], sq, c == 0, c == 7, R=[sk, "ones16"], W=[("ps", SB_)])
                rs, rk = rpool.get()
                rstd_from(ps[:, SB_, :], 1.0 / 1024.0, [("ps", SB_), "epsb"], rs, rk)
                for c in range(8):
                    tmp, tk = fpool.get()
                    stt(tmp, xT[:, c, cs], A1[:, l, c, ck:ck + 1], rs, ALU.mult, ALU.mult,
                        R=[xk(c, j), rk] + modR, W=[tk])
                    act(hT[:, c, cs], tmp, AF.Identity, [tk] + modR, [hk(c, j)], bias=modv[:, l, c, ck:ck + 1])

            if DBG.get("stop") == "prenorm":
                return

            pipe = []

            def pipe_tick():
                for g_ in list(pipe):
                    try:
                        next(g_)
                    except StopIteration:
                        pipe.remove(g_)

            def pipe_flush():
                while pipe:
                    pipe_tick()

            def pipe_add(gen):
                try:
                    next(gen)
                    pipe.append(gen)
                except StopIteration:
                    pass

            def proj_fm(slot, n, j):
                b = ring()
                for kc in range(8):
                    mm(ps[:, b, :], wring[:, slot, kc, n * 128:(n + 1) * 128], hT[:, kc, j * 512:(j + 1) * 512],
                       kc == 0, kc == 7, R=[("w", slot), hk(kc, j)], W=[("ps", b)])
                pipe_tick()
                return b

            def proj_tm(slot, blk, ncols=512):
                b = ring()
                for kc in range(8):
                    mm(ps[:, b, :ncols], hT[:, kc, blk * 128:(blk + 1) * 128], wring[:, slot, kc, :ncols],
                       kc == 0, kc == 7, R=[("w", slot), hk(kc, blk // 4)], W=[("ps", b)])
                pipe_tick()
                return b

            def rope_gen(dst, dkey, src_ps_or_sb, srcR, j, raw_ready=None, N=512, tabs=None):
                if tabs is None:
                    cos_, sin_, tR = cosT[:, j * 512:(j + 1) * 512], sinT[:, j * 512:(j + 1) * 512], ["consts"]
                else:
                    cos_, sin_, tR = tabs
                if raw_ready is None:
                    raw, rk_ = bpool.get()
                    evac_copy(raw[:, :N], src_ps_or_sb, srcR, [rk_])
                    yield
                    yield
                else:
                    raw, rk_ = raw_ready
                b2 = aux()
                mm(ps[:, b2, :N], Rm, raw[:, :N], True, True, R=[rk_, "cb16", "cb16m"], W=[("ps", b2)])
                t1, k1 = fpool.get()
                tt(t1[:, :N], raw[:, :N], cos_, ALU.mult, [rk_] + tR, [k1])
                t2, k2 = fpool.get()
                tt(t2[:, :N], ps[:, b2, :N], sin_, ALU.mult, [("ps", b2)] + tR, [k2])
                tt(dst, t1[:, :N], t2[:, :N], ALU.add, [k1, k2], [dkey])

            def rope_to(dst, dkey, src_ps_or_sb, srcR, j, from_psum):
                pipe_add(rope_gen(dst, dkey, src_ps_or_sb, srcR, j))

            if even:
                vmemset(Q2[:, :, :], 0.0, [("q2", c, j) for c in range(4) for j in range(NT)] + ["q2pad"])
                lv = pcol(l, P_LAM, 256)
                for i2 in range(2):
                    pr, pk = fpool.get()
                    tt(pr[:, 0:64], lv[:, i2 * 128:i2 * 128 + 64], lv[:, i2 * 128 + 64:i2 * 128 + 128], ALU.mult,
                       ["params"], [pk])
                    act(pr[:, 64:128], pr[:, 0:64], AF.Identity, [pk], [pk, ("lamt", i2)], accum=lamt[:, i2:i2 + 1])
                    act(lamt[:, 2 + i2:3 + i2], lamt[:, i2:i2 + 1], AF.Exp, [("lamt", i2)], [("lamt", 2 + i2)])
                stt(lamt[:, 4:5], lamt[:, 2:3], -1.0, lamt[:, 3:4], ALU.mult, ALU.add,
                    [("lamt", 2), ("lamt", 3)], [("lamt", 4)])
                ts(lamt[:, 4:5], lamt[:, 4:5], -lam_init, None, ALU.add, None, [("lamt", 4)], [("lamt", 4)])
                ts(lamt[:, 5:6], pcol(l, P_SUBG, 1), 1.0 - lam_init, None, ALU.mult, None, ["params"], [("lamt", 5)])
                nlam = lamt[:, 4:5]
                sgs = lamt[:, 5:6]

                if DBG.get("stop") == "lam":
                    return
                PHASE[0] = f"{grp}{l}:inproj"
                for g in range(7):
                    if DBG.get("stop") == "g%d" % g:
                        return
                    slot = w_next()
                    if g < 2:
                        for pair in range(2):
                            c = g * 2 + pair
                            for j in range(NT):
                                bg = proj_fm(slot, pair * 2, j)
                                sg_, sgk = fpool.get()
                                act(sg_, ps[:, bg, :], AF.Sigmoid, [("ps", bg)], [sgk])
                                bu = proj_fm(slot, pair * 2 + 1, j)
                                if isS:
                                    tt(Q2[:, c, 15 + j * 512:15 + (j + 1) * 512], ps[:, bu, :], sg_, ALU.mult,
                                       [("ps", bu), sgk], [("q2", c, j)])
                                else:
                                    tt(Q2[:, c, 0:572].rearrange("p (s t) -> p s t", s=2)[:, :, 15:271],
                                       ps[:, bu, :].rearrange("p (s t) -> p s t", s=2),
                                       sg_.rearrange("p (s t) -> p s t", s=2), ALU.mult,
                                       [("ps", bu), sgk], [("q2", c, j)])
                    elif g in (2, 3):
                        Z = Z1 if g == 2 else Z2
                        zn = "z1" if g == 2 else "z2"
                        for n in range(4):
                            for j in range(NT):
                                b = proj_fm(slot, n, j)
                                act(Z[:, n, j * 512:(j + 1) * 512], ps[:, b, :], AF.Silu, [("ps", b)], [(zn, n, j)])
                    elif g == 4:
                        for n in range(4):
                            for j in range(NT):
                                b = proj_fm(slot, n, j)
                                if isS:
                                    rope_to(Q1[:, n, j * 512:(j + 1) * 512], ("q1", n, j), ps[:, b, :], [("ps", b)], j, True)
                                else:
                                    evac_copy(Q1[:, n, j * 512:(j + 1) * 512], ps[:, b, :], [("ps", b)], [("q1", n, j)])
                    elif g == 5:
                        for n in range(4):
                            for j in range(NT):
                                b = proj_fm(slot, n, j)
                                dst = KB[:, n, knew0 + j * 512:knew0 + (j + 1) * 512]
                                if isS:
                                    rope_to(dst, ("k", n, 1 + j), ps[:, b, :], [("ps", b)], j, True)
                                else:
                                    evac_copy(dst, ps[:, b, :], [("ps", b)], [("k", n, j)])
                        if not isS:
                            for blk in list(range(4)) * DBG.get("rep5", 1):
                                b = proj_tm(slot, blk)
                                o_, ok = opool.get()
                                evac_copy(o_, ps[:, b, :], [("ps", b)], [ok])
                                dma("sp", nbk_d[blk // 2, li, (blk % 2) * 128:(blk % 2 + 1) * 128, :], o_, [ok], [])
                    else:
                        for blk in range(T // 128):
                            b = proj_tm(slot, blk)
                            if not DBG.get("skipvb"):
                                evac_copy(VB[:, vnew0 + blk, 0:512], ps[:, b, :], [("ps", b)], [("v", vnew0 + blk)])
                            if not isS and not DBG.get("skipnbv"):
                                o_, ok = opool.get()
                                evac_copy(o_, ps[:, b, :], [("ps", b)], [ok])
                                dma("sp", nbv_d[blk // 2, li, (blk % 2) * 128:(blk % 2 + 1) * 128, :], o_, [ok], [])
                pipe_flush()
                if isS:
                    dma("sp", kstage[:], cbk_d[li].rearrange("(b p) f -> p b f", p=128), R=(), W=SK4)
                    for c in range(4):
                        b = ring()
                        for blk in range(4):
                            tr(ps[:, b, blk * 128:(blk + 1) * 128], kstage[:, blk, c * 128:(c + 1) * 128],
                               R=SK4 + ["consts"], W=[("ps", b)])
                        evac_copy(KB[:, c, 0:512], ps[:, b, :], [("ps", b)], [("k", c, 0)])
                    dma("pool", VB[:, 0:4, 0:512], cbv_d[li].rearrange("(b p) f -> p b f", p=128), R=(),
                        W=[("v", i) for i in range(4)])

                if DBG.get("stop") == "inproj":
                    return
                PHASE[0] = f"{grp}{l}:conv"
                tiles = []
                if isS:
                    tiles = [(0, 512), (512, 512)]
                else:
                    tiles = [(0, 256), (256, 256)]
                for c in range(4):
                    banks = [4 + (c % 2) * 2, 5 + (c % 2) * 2]
                    for k in range(CONV_K):
                        dg, dk_ = dpool.get()
                        ts(dg, cb16[:, 4, :], pcol(l, P_CONVW + c * 31 + k, 1), None, ALU.mult, None,
                           ["cb16", "cb16m", "params"], [dk_])
                        for ti, (c0, N) in enumerate(tiles):
                            a0 = apad_base(c0) - 15 + k
                            mm(ps[:, banks[ti], :N], dg, Q2[:, c, a0:a0 + N], k == 0, k == CONV_K - 1,
                               R=[dk_, "q2pad"] + [("q2", c, j) for j in range(NT)], W=[("ps", banks[ti])])
                    for ti, (c0, N) in enumerate(tiles):
                        act(cvbuf[:, c, c0:c0 + N], ps[:, banks[ti], :N], AF.Identity, [("ps", banks[ti]), "params"],
                            [("ycv", 2 * c + c0 // 512)], bias=pcol(l, P_CONVB + c, 1))
                ts(lnneg[:, 0:8], pcol(l, P_LNG, 8), -1.0, None, ALU.mult, None, ["params"], [("lnneg", l)])
                ln_extra = []
                for ti, (c0, N) in enumerate(tiles):
                    lst = {}

                    def ln_stats(lst=lst, c0=c0, N=N):
                        bA, bB = ring(), ring()
                        for c in range(4):
                            mm(ps[:, bA, :N], ones32[:], cvbuf[:, c, c0:c0 + N], c == 0, c == 3,
                               R=[("ycv", 2 * c + c0 // 512), "ones32"], W=[("ps", bA)])
                        for c in range(4):
                            sq, sk = fpool.get()
                            act(sq[:, :N], cvbuf[:, c, c0:c0 + N], AF.Square, [("ycv", 2 * c + c0 // 512)], [sk])
                            mm(ps[:, bB, :N], ones32[:], sq[:, :N], c == 0, c == 3, R=[sk, "ones32"], W=[("ps", bB)])
                        lst["bA"], lst["bB"] = bA, bB

                    def ln_rstd(lst=lst, N=N):
                        bA, bB = lst["bA"], lst["bB"]
                        mean, mnk = rpool.get()
                        act(mean[:, :N], ps[:, bA, :N], AF.Copy, [("ps", bA)], [mnk])
                        m2, mk = fpool.get()
                        act(m2[:, :N], ps[:, bA, :N], AF.Square, [("ps", bA)], [mk])
                        var, vk = fpool.get()
                        tt(var[:, :N], ps[:, bB, :N], m2[:, :N], ALU.subtract, [("ps", bB), mk], [vk])
                        rs, rk = rpool.get()
                        rstd_from(var[:, :N], 1.0, [vk, "epsb"], rs[:, :N], rk, n=N)
                        lst.update(mean=mean, mnk=mnk, rs=rs, rk=rk)

                    def ln_chunk(c, lst=lst, c0=c0, N=N):
                        mean, mnk, rs, rk = lst["mean"], lst["mnk"], lst["rs"], lst["rk"]
                        t1, k1 = fpool.get()
                        tt(t1[:, :N], cvbuf[:, c, c0:c0 + N], mean[:, :N], ALU.subtract,
                           [("ycv", 2 * c + c0 // 512), mnk], [k1])
                        tt(t1[:, :N], t1[:, :N], rs[:, :N], ALU.mult, [k1, rk], [k1])
                        t2, k2 = fpool.get()
                        act(t2[:, :N], t1[:, :N], AF.Exp, [k1, ("lnneg", l)], [k2], scale=lnneg[:, c:c + 1],
                            bias=lnneg[:, 4 + c:5 + c])
                        act(t2[:, :N], t2[:, :N], AF.Ln, [k2, "epsb"], [k2], bias=epsb[:, 1:2])
                        act(t2[:, :N], t2[:, :N], AF.Exp, [k2], [k2], scale=-1.0)
                        ts(t1[:, :N], t1[:, :N], pcol(l, P_LNG + c, 1), pcol(l, P_LNB + c, 1), ALU.mult, ALU.add,
                           [k1, "params"], [k1])
                        tt(t1[:, :N], t1[:, :N], t2[:, :N], ALU.mult, [k1, k2], [k1])
                        tt(hT[:, c, c0:c0 + N], t1[:, :N], Z1[:, c, c0:c0 + N], ALU.mult,
                           [k1, ("z1", c, c0 // 512)], [hk(c, c0 // 512)])

                    base = ti * 6
                    ln_extra.append((base, ln_stats))
                    ln_extra.append((base + 1, ln_rstd))
                    for c in range(4):
                        ln_extra.append((base + 2 + c, (lambda c=c, f=ln_chunk: f(c))))

                if DBG.get("stop") == "apath":
                    return
                PHASE[0] = f"{grp}{l}:attn"
                stream = []
                OB = (4, 5)
                DB = (6, 7)
                QN = 256
                subtiles = [(u, qt) for u in units for qt in range(u["qn"] // QN)]
                for g0 in range(0, len(subtiles), 2):
                    grp_ = subtiles[g0:g0 + 2]
                    gq0 = grp_[0][0]["q0"] + grp_[0][1] * QN
                    gs = slice(gq0, gq0 + 512)
                    jq = gq0 // 512
                    for h in range(4):
                        for sub, (u, qt) in enumerate(grp_):
                            q0 = u["q0"] + qt * QN
                            qs = slice(q0, q0 + QN)
                            a0 = sub * QN
                            nkb = len(u["kcols"])
                            for kbi, kc0 in enumerate(u["kcols"]):
                                kkey = ("k", h, (0 if kc0 < 512 else 1 + (kc0 - 512) // 512)) if isS else ("k", h, 0)
                                vb_ = u["vblk"][kbi]
                                for cc in range(2):
                                    def mm_(b, c0, kc0=kc0, cc=cc, h=h, qs=qs, kkey=kkey, jq=jq):
                                        pr_ = slice(cc * 64, cc * 64 + 64)
                                        mm(ps[:, b, c0:c0 + QN], KB[pr_, h, kc0:kc0 + 128], Q1[pr_, h, qs], True, True,
                                           R=[kkey, ("q1", h, jq)], W=[("ps", b)])

                                    def pv(pT, pk, c0, cc=cc, h=h, vb_=vb_, a0=a0, first=(kbi == 0), last=(kbi == nkb - 1)):
                                        mm(ps[:, OB[cc], a0:a0 + QN], VB[:, vb_, h * 128:(h + 1) * 128], pT[:, c0:c0 + QN],
                                           first, last, R=[pk, ("v", vb_)], W=[("ps", OB[cc])])
                                        mm(ps[:, DB[cc], a0:a0 + QN], ones16[:], pT[:, c0:c0 + QN], first, last,
                                           R=[pk, "ones16"], W=[("ps", DB[cc])])

                                    fins = []
                                    if sub == len(grp_) - 1 and kbi == nkb - 1 and cc == 1:
                                        fst = {}
                                        FN = 512

                                        def fin1(fst=fst, FN=FN):
                                            tl = []
                                            for c2 in range(2):
                                                ln_, lk = fpool.get()
                                                act(ln_[:, :FN], ps[:, DB[c2], :FN], AF.Ln, [("ps", DB[c2])], [lk])
                                                act(ln_[:, :FN], ln_[:, :FN], AF.Exp, [lk], [lk], scale=-1.0)
                                                t_, tk_ = fpool.get()
                                                tt(t_[:, :FN], ps[:, OB[c2], :FN], ln_[:, :FN], ALU.mult,
                                                   [("ps", OB[c2]), lk], [tk_])
                                                tl.append((t_, tk_))
                                            dl, dk2 = dlpool.get()
                                            stt(dl[:, :FN], tl[1][0][:, :FN], nlam, tl[0][0][:, :FN], ALU.mult, ALU.add,
                                                [tl[0][1], tl[1][1], ("lamt", 4)], [dk2])
                                            sq, sk = sqpool.get()
                                            act(sq[:, :FN], dl[:, :FN], AF.Square, [dk2], [sk])
                                            fst.update(dl=dl, dk2=dk2, sq=sq, sk=sk)

                                        def fin2(fst=fst, h=h, gs=gs, jq=jq, FN=FN):
                                            dl, dk2, sq, sk = fst["dl"], fst["dk2"], fst["sq"], fst["sk"]
                                            b = ring()
                                            ring()
                                            mm(ps[:, b, :FN], ones16[:], sq[:, :FN], True, True, R=[sk, "ones16"],
                                               W=[("ps", b)])
                                            rs, rk = fpool.get()
                                            rstd_from(ps[:, b, :FN], 1.0 / 128.0, [("ps", b), "epsb"], rs[:, :FN], rk, n=FN)
                                            stt(dl[:, :FN], dl[:, :FN], sgs, rs[:, :FN], ALU.mult, ALU.mult,
                                                [dk2, rk, ("lamt", 5)], [dk2])
                                            tt(hT[:, 4 + h, gs], dl[:, :FN], Z2[:, h, gs], ALU.mult, [dk2, ("z2", h, jq)],
                                               [hk(4 + h, jq)])

                                        fins = [(0, fin1), (1, fin2)]
                                    stream.append(dict(w=QN, mm=mm_, post=None, pv=pv, fins=fins))
                run_stream(stream, 1, extra=ln_extra + next_mod)
            else:
                act(esink[:], pcol(l, P_SINK, 8), AF.Exp, ["params"], ["esink"])
                for reg in range(2):
                    vmemset(VB[:, :, reg * 320:reg * 320 + 320].rearrange("p b (t e) -> p b t e", e=64)[:, :, 0::2, :],
                            1.0, ["vones"] + [("v", i) for i in range(12)])

                def headnorm_gen(dst, dkey, b, gain, j, N=512, tabs=None):
                    sq, sk = bpool.get()
                    act(sq[:, :N], ps[:, b, :N], AF.Square, [("ps", b)], [sk])
                    yield
                    b2 = aux()
                    mm(ps[:, b2, :N], BD, sq[:, :N], True, True, R=[sk, "cb16", "cb16m"], W=[("ps", b2)])
                    rs, rk = fpool.get()
                    rstd_from(ps[:, b2, :N], 1.0, [("ps", b2), "epsb"], rs[:, :N], rk, n=N)
                    if isS:
                        qn_, qk = bpool.get()
                        stt(qn_[:, :N], ps[:, b, :N], gain, rs[:, :N], ALU.mult, ALU.mult, [("ps", b), rk, "params"], [qk])
                        yield
                        yield
                        yield from rope_gen(dst, dkey, None, None, j, raw_ready=(qn_, qk), N=N, tabs=tabs)
                    else:
                        stt(dst, ps[:, b, :N], gain, rs[:, :N], ALU.mult, ALU.mult, [("ps", b), rk, "params"], [dkey])

                def headnorm_to(dst, dkey, b, gain, j):
                    pipe_add(headnorm_gen(dst, dkey, b, gain, j))

                PHASE[0] = f"{grp}{l}:inproj"
                def tm_block(slot, blk):
                    ncols = 256 if isS else 512
                    b = proj_tm(slot, blk, ncols)
                    for reg in range(2):
                        evac_copy(VB[:, vnew0 + blk, reg * 320:reg * 320 + 320]
                                  .rearrange("p (t e) -> p t e", e=64)[:, 1::2, :],
                                  ps[:, b, reg * 128:reg * 128 + 128].rearrange("p (t e) -> p t e", e=64),
                                  [("ps", b), "vones"], [("v", vnew0 + blk, reg)])
                    if not isS:
                        o_, ok = opool.get()
                        evac_copy(o_, ps[:, b, :], [("ps", b)], [ok])
                        sq_, bi = blk // 2, (blk % 2) * 128
                        dma("sp", ncv_d[sq_, li, bi:bi + 128, :], o_[:, 0:128], [ok], [])
                        dma("sp", ndv_d[sq_, li, bi:bi + 128, :], o_[:, 128:256], [ok], [])
                        dma("sp", ndk_d[sq_, li, bi:bi + 128, :], o_[:, 384:512], [ok], [])
                        o2, ok2 = opool.get()
                        sm, smk = spool.get()
                        for hh in range(2):
                            act(o2[:, hh * 64:(hh + 1) * 64], o_[:, 256 + hh * 64:256 + (hh + 1) * 64], AF.Square,
                                [ok], [ok2, smk], accum=sm[:, hh:hh + 1])
                        act(sm[:, 0:2], sm[:, 0:2], AF.Ln, [smk, "epsb"], [smk], scale=1.0 / 64.0, bias=epsb[:, 0:1])
                        act(sm[:, 0:2], sm[:, 0:2], AF.Exp, [smk], [smk], scale=-0.5)
                        for hh in range(2):
                            stt(o2[:, hh * 64:(hh + 1) * 64], o_[:, 256 + hh * 64:256 + (hh + 1) * 64],
                                sm[:, hh:hh + 1], pcol(l, P_KNF, 64), ALU.mult, ALU.mult,
                                [ok, smk, "params"], [ok2])
                        dma("sp", nck_d[sq_, li, bi:bi + 128, :], o2[:, 0:128], [ok2], [])

                if last:
                    h_own = cvbuf[:, 0, :].bitcast(BF16).rearrange("p (c t) -> p c t", c=8)
                    HOK = [("ycv", 0), ("ycv", 1)]
                    for kc in range(8):
                        dst = h_own[:, kc, :]
                        ts(dst, hT[:, kc, 0:256], selt[:, 0:1], None, ALU.mult, None, [hk(kc, 0), "sel"], HOK)
                        for r_ in range(1, 4):
                            stt(dst, hT[:, kc, r_ * 256:(r_ + 1) * 256], selt[:, r_:r_ + 1], dst, ALU.mult, ALU.add,
                                [hk(kc, r_ // 2), "sel"] + HOK, HOK)
                    dma("sp", stage[:, 1, 0:512], ropeown_d, R=(), W=[("stage", 1, 0)])
                    own_tabs = (stage[:, 1, 0:256], stage[:, 1, 256:512], [("stage", 1, 0)])

                    def proj_own(slot, n):
                        b = ring()
                        for kc in range(8):
                            mm(ps[:, b, :256], wring[:, slot, kc, n * 128:(n + 1) * 128], h_own[:, kc, :],
                               kc == 0, kc == 7, R=[("w", slot)] + HOK, W=[("ps", b)])
                        pipe_tick()
                        return b

                def emit_qz():
                    for g in range(4):
                        slot = w_next()
                        Z = Z1 if g < 2 else Z2
                        zn = "z1" if g < 2 else "z2"
                        for pair in range(2):
                            c = (g % 2) * 2 + pair
                            if last:
                                b = proj_own(slot, pair * 2)
                                if g < 2:
                                    pipe_add(headnorm_gen(Q1[:, c, 0:256], ("q1", c, 0), b, pcol(l, P_QN, 1), 0, N=256,
                                                          tabs=own_tabs))
                                else:
                                    pipe_add(rope_gen(Q2[:, c, 0:256], ("q2", c, 0), ps[:, b, :256], [("ps", b)], 0, N=256,
                                                      tabs=own_tabs))
                                continue
                            for j in range(NT):
                                b = proj_fm(slot, pair * 2, j)
                                if g < 2:
                                    headnorm_to(Q1[:, c, j * 512:(j + 1) * 512], ("q1", c, j), b, pcol(l, P_QN, 1), j)
                                else:
                                    dst = Q2[:, c, j * 512:(j + 1) * 512]
                                    if isS:
                                        rope_to(dst, ("q2", c, j), ps[:, b, :], [("ps", b)], j, True)
                                    else:
                                        evac_copy(dst, ps[:, b, :], [("ps", b)], [("q2", c, j)])
                        for pair in range(2):
                            c = (g % 2) * 2 + pair
                            if last:
                                b = proj_own(slot, pair * 2 + 1)
                                act(Z[:, c, 0:256], ps[:, b, :256], AF.Silu, [("ps", b)], [(zn, c, 0)])
                                continue
                            for j in range(NT):
                                b = proj_fm(slot, pair * 2 + 1, j)
                                act(Z[:, c, j * 512:(j + 1) * 512], ps[:, b, :], AF.Silu, [("ps", b)], [(zn, c, j)])

                def emit_kv():
                    slot = w_next()
                    slot5 = w_next(issue=False)
                    nblk = T // 128
                    for n in range(4):
                        for j in range(NT):
                            b = proj_fm(slot, n, j)
                            dst = KB[:, n, knew0 + j * 512:knew0 + (j + 1) * 512]
                            dkey = ("k", n, 1 + j) if isS else ("k", n, j)
                            if n < 2:
                                headnorm_to(dst, dkey, b, pcol(l, P_KN, 1), j)
                            elif isS:
                                rope_to(dst, dkey, ps[:, b, :], [("ps", b)], j, True)
                            else:
                                evac_copy(dst, ps[:, b, :], [("ps", b)], [dkey])
                        for blk in range(n * nblk // 4, (n + 1) * nblk // 4):
                            tm_block(slot5, blk)

                if last:
                    emit_kv()
                    emit_qz()
                else:
                    emit_qz()
                    emit_kv()
                pipe_flush()
                if isS:
                    for mi, (ck_d_, cv_d_) in enumerate(((cck_d, ccv_d), (cdk_d, cdv_d))):
                        for kv in range(2):
                            for dup in range(2):
                                dma("sp", kstage[:, :, mi * 256 + kv * 128 + dup * 64: mi * 256 + kv * 128 + dup * 64 + 64],
                                    ck_d_[li].rearrange("(b p) f -> p b f", p=128)[:, :, kv * 64:(kv + 1) * 64],
                                    R=(), W=SK4)
                        for kv in range(2):
                            b = ring()
                            for blk in range(4):
                                tr(ps[:, b, blk * 128:(blk + 1) * 128],
                                   kstage[:, blk, mi * 256 + kv * 128: mi * 256 + kv * 128 + 128],
                                   R=SK4 + ["consts"], W=[("ps", b)])
                            evac_copy(KB[:, mi * 2 + kv, 0:512], ps[:, b, :], [("ps", b)], [("k", mi * 2 + kv, 0)])
                        for kv in range(2):
                            dma("pool", VB[:, 0:4, mi * 320 + 64 + kv * 128:mi * 320 + 128 + kv * 128],
                                cv_d_[li].rearrange("(b p) f -> p b f", p=128)[:, :, kv * 64:(kv + 1) * 64], R=["vones"],
                                W=[("v", i, mi, kv) for i in range(4)])

                PHASE[0] = f"{grp}{l}:attn"
                att_units = units
                if last:
                    def blend(buf, c, key):
                        dst = buf[:, c, 0:256]
                        ts(dst, buf[:, c, 0:256], selt[:, 0:1], None, ALU.mult, None, [key(c, 0), "sel"], [key(c, 0)])
                        for r_ in range(1, 4):
                            stt(dst, buf[:, c, r_ * 256:(r_ + 1) * 256], selt[:, r_:r_ + 1], dst, ALU.mult, ALU.add,
                                [key(c, r_ // 2), key(c, 0), "sel"], [key(c, 0)])
                    for c in range(8):
                        blend(xT, c, lambda c_, j_: xk(c_, j_))
                    for hf in range(2):
                        dma("pool", stage[:, 0, hf * 512:(hf + 1) * 512].bitcast(BF16), dmask_d[:, hf * 1024:(hf + 1) * 1024],
                            R=(), W=[("stage", 0, hf)])
                    att_units = [dict(q0=0, qn=256, kcols=[i * 128 for i in range(12)], vblk=list(range(12)), seq=0)]
                acc_i = [0]
                stream = []
                QN = 256
                subtiles = [(u, qt) for u in att_units for qt in range(u["qn"] // QN)]
                for mi in range(2):
                    Qb = Q1 if mi == 0 else Q2
                    qn_ = "q1" if mi == 0 else "q2"
                    Zb = Z1 if mi == 0 else Z2
                    zn = "z1" if mi == 0 else "z2"
                    for g0 in range(0, len(subtiles), 2):
                        grp_ = subtiles[g0:g0 + 2]
                        gq0 = grp_[0][0]["q0"] + grp_[0][1] * QN
                        GW = QN * len(grp_)
                        gs = slice(gq0, gq0 + GW)
                        jq = gq0 // 512
                        for c in range(4):
                            ABs = []
                            for half in range(2):
                                ABs.append(4 + acc_i[0] % 4)
                                acc_i[0] += 1
                            for sub, (u, qt) in enumerate(grp_):
                                q0 = u["q0"] + qt * QN
                                a0 = sub * QN
                                steps = []
                                if mi == 1 and last:
                                    for kbi in range(4):
                                        steps.append(([(kbi * 128, kbi, None)], 0, QN))
                                    for kb in range(8):
                                        steps.append(([(512 + kb * 128, 4 + kb, dmtab[:, kb, :])], 0, QN))
                                elif mi == 1 and isS:
                                    for kbi in range(4):
                                        steps.append(([(kbi * 128, kbi, None)], 0, QN))
                                    for qb in range(QN // 128):
                                        gq = qt * (QN // 128) + qb
                                        subs = []
                                        for kb in (gq - 1, gq, gq + 1):
                                            if 0 <= kb < 8:
                                                msk = None if kb == gq else (MN if kb == gq + 1 else MP)
                                                subs.append((512 + kb * 128, 4 + kb, msk))
                                        steps.append((subs, qb * 128, 128))
                                else:
                                    for kbi, kc0 in enumerate(u["kcols"]):
                                        steps.append(([(kc0, u["vblk"][kbi], None)], 0, QN))
                                for si, (subs, s0, sn) in enumerate(steps):
                                    for half in range(2):
                                        h = c * 2 + half
                                        kv = h // 4
                                        kchunk = mi * 2 + kv
                                        AB = ABs[half]
                                        vc0 = mi * 320 + (64 + 128 * kv if half == 0 else 128 * kv)

                                        def mm_(b, c0, half=half, kchunk=kchunk, subs=subs, Qb=Qb, c=c, q0=q0, s0=s0, sn=sn,
                                                qn_=qn_, jq=jq):
                                            pr_ = slice(half * 64, half * 64 + 64)
                                            for i_, (kc0, vb, msk) in enumerate(subs):
                                                kkey = ("k", kchunk, (0 if kc0 < 512 else 1 + (kc0 - 512) // 512)) if isS \
                                                    else ("k", kchunk, 0)
                                                mm(ps[:, b, c0 + i_ * sn:c0 + (i_ + 1) * sn], KB[pr_, kchunk, kc0:kc0 + 128],
                                                   Qb[pr_, c, q0 + s0:q0 + s0 + sn], True, True,
                                                   R=[kkey, (qn_, c, jq)], W=[("ps", b)])

                                        post = None
                                        if any(m is not None for (_, _, m) in subs):
                                            def post(pT, pk, c0, subs=subs, sn=sn):
                                                for i_, (kc0, vb, msk) in enumerate(subs):
                                                    if msk is not None:
                                                        tt(pT[:, c0 + i_ * sn:c0 + (i_ + 1) * sn],
                                                           pT[:, c0 + i_ * sn:c0 + (i_ + 1) * sn], msk, ALU.mult,
                                                           [pk, "cb16", "cb16m"] + DMK, [pk])

                                        def pv(pT, pk, c0, AB=AB, s0=s0, sn=sn, subs=subs, vc0=vc0, mi=mi, kv=kv, a0=a0,
                                               first=(si == 0), last=(si == len(steps) - 1)):
                                            for i_, (kc0, vb, msk) in enumerate(subs):
                                                mm(ps[:, AB, a0 + s0:a0 + s0 + sn], VB[:, vb, vc0:vc0 + 128],
                                                   pT[:, c0 + i_ * sn:c0 + (i_ + 1) * sn],
                                                   first and i_ == 0, last and i_ == len(subs) - 1,
                                                   R=[pk, ("v", vb, mi), ("v", vb, mi, kv), ("v", vb), "vones"], W=[("ps", AB)])

                                        fins = []
                                        if sub == len(grp_) - 1 and si == len(steps) - 1:
                                            def fin(half=half, AB=AB, mi=mi, h=h, Zb=Zb, zn=zn, c=c, gs=gs, jq=jq, GW=GW):
                                                FN = GW
                                                pr_ = slice(half * 64, half * 64 + 64)
                                                dr = slice(64, 128) if half == 0 else slice(0, 64)
                                                rr, rk = fpool.get()
                                                if mi == 1:
                                                    act(rr[dr, :FN], ps[dr, AB, :FN], AF.Ln, [("ps", AB), "esink"], [rk],
                                                        bias=esink[dr, h:h + 1])
                                                else:
                                                    act(rr[dr, :FN], ps[dr, AB, :FN], AF.Ln, [("ps", AB)], [rk])
                                                act(rr[pr_, :FN], rr[dr, :FN], AF.Exp, [rk], [rk], scale=-1.0)
                                                tt(rr[pr_, :FN], rr[pr_, :FN], Zb[pr_, c, gs], ALU.mult, [rk, (zn, c, jq)], [rk])
                                                tt(hT[pr_, mi * 4 + c, gs], ps[pr_, AB, :FN], rr[pr_, :FN], ALU.mult,
                                                   [("ps", AB), rk], [hk(mi * 4 + c, jq)])
                                            fins = [(0, fin)]
                                        stream.append(dict(w=len(subs) * sn, mm=mm_, post=post, pv=pv, fins=fins))
                run_stream(stream, 1, extra=next_mod)

            if DBG.get("stop") == "attn":
                return
            PHASE[0] = f"{grp}{l}:outproj"
            s0_ = w_next()
            s1_ = w_next(issue=False)
            SB2 = 6
            for (oc0, ON) in ([(0, 256)] if last else [(j_ * 512, 512) for j_ in range(NT)]):
                j = oc0 // 512
                cs = slice(oc0, oc0 + ON)
                prev_sq = None
                for fo in range(8):
                    slot = s0_ if fo < 4 else s1_
                    b = ring()
                    for kc in range(8):
                        mm(ps[:, b, :ON], wring[:, slot, kc, (fo % 4) * 128:(fo % 4 + 1) * 128], hT[:, kc, cs],
                           kc == 0, kc == 7, R=[("w", slot), hk(kc, j)], W=[("ps", b)])
                    if prev_sq is not None:
                        mm(ps[:, SB2, :ON], ones16[:], prev_sq[0][:, :ON], prev_sq[2] == 0, False, R=[prev_sq[1], "ones16"],
                           W=[("ps", SB2)])
                    act(ybuf[:, fo, :ON], ps[:, b, :ON], AF.Copy, [("ps", b)], [("ycv", fo)])
                    sq, sk = bpool.get()
                    act(sq[:, :ON], ps[:, b, :ON], AF.Square, [("ps", b)], [sk])
                    prev_sq = (sq, sk, fo)
                mm(ps[:, SB2, :ON], ones16[:], prev_sq[0][:, :ON], False, True, R=[prev_sq[1], "ones16"], W=[("ps", SB2)])
                rs, rk = rpool.get()
                rstd_from(ps[:, SB2, :ON], 1.0 / 1024.0, [("ps", SB2), "epsb"], rs[:, :ON], rk, n=ON)
                for c in range(8):
                    tmp, tk = fpool.get()
                    tt(tmp[:, :ON], ybuf[:, c, :ON], rs[:, :ON], ALU.mult, [("ycv", c), rk], [tk])
                    stt(xT[:, c, cs], tmp[:, :ON], G1[:, l, c, ck:ck + 1], xT[:, c, cs], ALU.mult, ALU.add,
                        [tk, xk(c, j)] + modR, [xk(c, j)])

        if (not isS) and hoist:
            PHASE[0] = "S:prologue2"
            prologue_blocks(xs_d, range(4, 8))
        PHASE[0] = f"{grp}:epilogue"
        own_only = isS and DBG["layers"] == 4
        for blk in range(2 if own_only else min(T // 128, 2 if isS else 99)):
            st = blk % 2
            for half in range(2):
                b = ring()
                for q in range(4):
                    fc = half * 4 + q
                    tr(ps[:, b, q * 128:(q + 1) * 128], xT[:, fc, blk * 128:(blk + 1) * 128],
                       R=[xk(fc, blk // 4), "consts"], W=[("ps", b)])
                evac_copy(stage[:, st, half * 512:(half + 1) * 512], ps[:, b, :], [("ps", b)], [("stage", st, half)])
            dma("sp", y_d[blk * 128:(blk + 1) * 128, :], stage[:, st, :], R=[("stage", st, 0), ("stage", st, 1)], W=[])

    for g_ in DBG["passes"]:
        run_pass(g_)
    S.finalize()
    LAST["S"] = S

    with nc.Block() as block:
        @block.sync
        def _(e):
            S.emit("sp", e, sems, final_wait=True)

        @block.gpsimd
        def _(e):
            S.emit("pool", e, sems, final_wait=True)

        @block.tensor
        def _(e):
            S.emit("pe", e, sems)

        @block.scalar
        def _(e):
            S.emit("act", e, sems)

        @block.vector
        def _(e):
            S.emit("dve", e, sems)

    es.close()
    return nc


def _fm(v, nch):
    return np.ascontiguousarray(np.asarray(v, np.float32).reshape(nch, 128).T)


def _wgroups(W, cols):
    Wc = np.asarray(W, np.float32)[:, cols]
    G = Wc.shape[1] // 512
    return np.ascontiguousarray(Wc.reshape(8, 128, G, 512).transpose(2, 1, 0, 3))


def _even_cols():
    ar = np.arange
    cols = []
    for c in range(4):
        cols += [ar(512 + c * 128, 512 + (c + 1) * 128), ar(c * 128, (c + 1) * 128)]
    cols += [ar(1024, 1536), ar(3072, 3584), ar(1536, 2048), ar(2048, 2560), ar(2560, 3072)]
    return np.concatenate(cols)


def _odd_cols():
    ar = np.arange
    cq, cz, dq, dz = 0, 768, 1280, 2048
    cols = []
    for q0, z0 in ((cq, cz), (dq, dz)):
        for c in range(4):
            cols += [ar(q0 + c * 128, q0 + (c + 1) * 128), ar(z0 + c * 128, z0 + (c + 1) * 128)]
    ck, dk = 512, 1792
    cols += [ar(ck, ck + 64), ar(ck, ck + 64), ar(ck + 64, ck + 128), ar(ck + 64, ck + 128)]
    cols += [ar(dk, dk + 64), ar(dk, dk + 64), ar(dk + 64, dk + 128), ar(dk + 64, dk + 128)]
    cols += [ar(640, 768), ar(1920, 2048), ar(512, 640), ar(1792, 1920)]
    return np.concatenate(cols)


def _consts():
    c = np.zeros((128, NCONST), np.float32)
    cm = np.zeros((128, 4, 128), np.float32)
    p = np.arange(128)
    c[:, C_ID:C_ID + 128] = np.eye(128, dtype=np.float32)
    d = p % 64
    partner = np.where((d // 16) % 2 == 0, p + 16, p - 16)
    R = np.zeros((128, 128), np.float32)
    R[partner, p] = 1.0
    cm[:, 0] = R
    cm[:, 1] = (p[:, None] // 64 == p[None, :] // 64).astype(np.float32) / 64.0
    cm[:, 2] = (p[:, None] <= p[None, :]).astype(np.float32)
    cm[:, 3] = (p[None, :] <= p[:, None]).astype(np.float32)
    t = np.arange(DEC_SEQ)
    row = (t // 64).astype(np.float32)
    col = (t % 64).astype(np.float32)
    nf = 16
    inv = (np.float32(10000.0) ** (-np.arange(nf, dtype=np.float32) / np.float32(nf))).astype(np.float32)
    a = (d // 32)
    j = d % 16
    b = (d // 16) % 2
    pos = np.where(a[:, None] == 0, row[None, :], col[None, :]).astype(np.float32)
    ang = (pos * inv[j][:, None]).astype(np.float32)
    c[:, C_COS:C_COS + 1024] = np.cos(ang)
    sgn = np.where(b == 0, -1.0, 1.0).astype(np.float32)
    c[:, C_SIN:C_SIN + 1024] = np.sin(ang) * sgn[:, None]
    return c, cm


def _dmask(r):
    k = np.arange(128)[:, None, None]
    kb = np.arange(8)[None, :, None]
    q = np.arange(256)[None, None, :]
    tq = r * 256 + q
    tk = kb * 128 + k
    return np.ascontiguousarray((np.abs(tq - tk) <= 128).astype(np.float32).reshape(128, 2048))


_NC_CACHE = {}


def kernel(x_prompt, x_sample, cache_b_k, cache_b_v, cache_c_k, cache_c_v, cache_d_k, cache_d_v,
           c, c_ctx, norm_pre, norm_post, w_mod, b_mod, w_in_even, a_conv_w, a_conv_b, a_ln_g,
           a_ln_b, b_lambda, b_subln_g, w_out_even, w_in_odd, c_q_norm, c_k_norm, d_sink, w_out_odd):
    f = lambda a: np.asarray(a, np.float32)
    x_prompt, x_sample = f(x_prompt), f(x_sample)
    ec, oc = _even_cols(), _odd_cols()
    wl = []
    for l in range(4):
        i = l // 2
        gm = _wgroups(w_mod[l], np.arange(3072))
        if l % 2 == 0:
            gi = _wgroups(w_in_even[i], ec)
            go = _wgroups(w_out_even[i], np.arange(1024))
        else:
            gi = _wgroups(w_in_odd[i], oc)
            go = _wgroups(w_out_odd[i], np.arange(1024))
        wl.append(np.ascontiguousarray(np.concatenate([gm, gi, go], axis=0)))
    params = np.zeros((128, 4 * NPL), np.float32)
    for l in range(4):
        i = l // 2
        o = l * NPL
        params[:, o + P_NPRE:o + P_NPRE + 8] = _fm(norm_pre[l], 8)
        params[:, o + P_NPOST:o + P_NPOST + 8] = _fm(norm_post[l], 8)
        params[:, o + P_BMOD:o + P_BMOD + 24] = _fm(b_mod[l], 24)
        if l % 2 == 0:
            cw = f(a_conv_w[i])
            params[:, o + P_CONVW:o + P_CONVW + 124] = cw.reshape(31, 4, 128).transpose(2, 1, 0).reshape(128, 124)
            params[:, o + P_CONVB:o + P_CONVB + 4] = _fm(a_conv_b[i], 4)
            params[:, o + P_LNG:o + P_LNG + 4] = _fm(a_ln_g[i], 4)
            params[:, o + P_LNB:o + P_LNB + 4] = _fm(a_ln_b[i], 4)
            params[:, o + P_SUBG] = f(b_subln_g[i])
            params[:, o + P_LAM:o + P_LAM + 256] = f(b_lambda[i]).reshape(1, 256)
        else:
            params[:, o + P_QN] = np.tile(f(c_q_norm[i]), 2)
            params[:, o + P_KN] = np.tile(f(c_k_norm[i]), 2)
            params[:, o + P_SINK:o + P_SINK + 8] = f(d_sink[i]).reshape(1, 8)
            params[:, o + P_KNF:o + P_KNF + 64] = f(c_k_norm[i]).reshape(1, 64)
    consts, cmat = _consts()
    cbk = f(cache_b_k).reshape(2, 2, 512, 512)
    cbv = f(cache_b_v).reshape(2, 2, 512, 512)
    cck = f(cache_c_k).reshape(2, 2, 512, 128)
    ccv = f(cache_c_v).reshape(2, 2, 512, 128)
    cdk = f(cache_d_k).reshape(2, 2, 512, 128)
    cdv = f(cache_d_v).reshape(2, 2, 512, 128)
    c, c_ctx = f(c), f(c_ctx)

    in_maps = []
    for core in range(N_CORES):
        s = core // 4
        cond = np.zeros((128, 16), np.float32)
        cond[:, 0::2] = _fm(c_ctx, 8)
        cond[:, 1::2] = _fm(c[s], 8)
        m = {
            "xp": np.ascontiguousarray(x_prompt[2 * core:2 * core + 2].reshape(512, 1024)),
            "xs": np.ascontiguousarray(x_sample[s]),
            "cond": cond, "params": params, "consts": consts, "cmat": cmat,
            "sel": np.ascontiguousarray(np.tile(np.eye(4, dtype=np.float32)[core % 4][None, :], (128, 1))),
            "dmask": _dmask(core % 4),
            "ropeown": np.ascontiguousarray(np.concatenate(
                [consts[:, C_COS + (core % 4) * 256:C_COS + (core % 4 + 1) * 256],
                 consts[:, C_SIN + (core % 4) * 256:C_SIN + (core % 4 + 1) * 256]], axis=1)),
            "cbk": np.ascontiguousarray(cbk[s]), "cbv": np.ascontiguousarray(cbv[s]),
            "cck": np.ascontiguousarray(cck[s]), "ccv": np.ascontiguousarray(ccv[s]),
            "cdk": np.ascontiguousarray(cdk[s]), "cdv": np.ascontiguousarray(cdv[s]),
        }
        for l in range(DBG["layers"]):
            m[f"wl{l}"] = wl[l]
        in_maps.append(m)

    if "nc" not in _NC_CACHE:
        _NC_CACHE["nc"] = build_program()
    nc = _NC_CACHE["nc"]
    res = run_bass_kernel_spmd(nc, in_maps, core_ids=list(range(N_CORES)))
    R = res.results
    y_prompt = np.concatenate([R[i]["yp"].reshape(2, 256, 1024) for i in range(8)], axis=0)
    y_sample = np.stack([np.concatenate([R[4 * s_ + r_]["ys"] for r_ in range(4)], axis=0) for s_ in range(2)], axis=0)
    cat = lambda k: np.concatenate([R[i][k] for i in range(8)], axis=0)
    new_b_k = cat("nbk").reshape(16, 2, 256, 4, 2, 64)
    new_b_v = cat("nbv").reshape(16, 2, 256, 4, 128)
    new_c_k = cat("nck").reshape(16, 2, 256, 2, 64)
    new_c_v = cat("ncv").reshape(16, 2, 256, 2, 64)
    new_d_k = cat("ndk").reshape(16, 2, 256, 2, 64)
    new_d_v = cat("ndv").reshape(16, 2, 256, 2, 64)
    return tuple(np.asarray(a, np.float32) for a in
                 (y_prompt, y_sample, new_b_k, new_b_v, new_c_k, new_c_v, new_d_k, new_d_v))
```

```python
import math
from contextlib import ExitStack

import numpy as np
import concourse.bass as bass
import concourse.mybir as mybir
from concourse.bass_utils import run_bass_kernel_spmd

F32 = mybir.dt.float32
BF16 = mybir.dt.bfloat16
AF = mybir.ActivationFunctionType
ALU = mybir.AluOpType

D_MODEL = 1024
DEPTH = 4
SEQ = 256
DEC_SEQ = 1024
PAST = 512
DH = 64
CONV_K = 31
EPS = 1e-6
N_CORES = 8
DBG = {"layers": 4, "passes": ("P", "S")}

P_NPRE, P_NPOST, P_BMOD, P_CONVW, P_CONVB, P_LNG, P_LNB, P_SUBG, P_LAM, P_QN, P_KN, P_SINK, P_KNF = (
    0, 8, 16, 40, 164, 168, 172, 176, 177, 433, 434, 435, 443)
NPL = 443 + 64
C_ID, C_COS, C_SIN = 0, 128, 1152
NCONST = 2176


PHASE = ["setup"]
LAST = {}


class Op:
    __slots__ = ("eng", "fn", "dma", "deps", "signal", "sigval", "slot", "slotval", "idx", "prewait", "phase")

    def __init__(self, eng, fn, dma):
        self.eng, self.fn, self.dma = eng, fn, dma
        self.deps = []
        self.signal = False
        self.sigval = 0
        self.slot = None
        self.slotval = 0
        self.idx = 0
        self.prewait = None


class Sched:
    COMPUTE = ("pe", "act", "dve")
    QUEUES = {"sp": 24, "pool": 12}

    def __init__(self):
        self.eng_ops = {e: [] for e in ("pe", "act", "dve", "sp", "pool")}
        self.last_w = {}
        self.readers = {}
        self.waited = {}
        self.dma_count = {q: 0 for q in self.QUEUES}
        self.slot_last = {}

    def add(self, eng, fn, R=(), W=(), dma=False):
        op = Op(eng, fn, dma)
        op.phase = PHASE[0]
        deps = []
        for b in R:
            w = self.last_w.get(b)
            if w is not None:
                deps.append(w)
            if isinstance(b, tuple) and b[0] == "ps":
                deps.extend(r for r in self.readers.get(b, ()) if r.eng != eng)
        for b in W:
            w = self.last_w.get(b)
            if w is not None:
                deps.append(w)
            deps.extend(self.readers.get(b, ()))
        lst = self.eng_ops[eng]
        op.idx = len(lst)
        seen = set()
        for d in deps:
            if id(d) in seen:
                continue
            seen.add(id(d))
            if d.dma:
                key = (eng, "dma", id(d))
                if key in self.waited:
                    continue
                self.waited[key] = True
                op.deps.append(d)
            else:
                if d.eng == eng and eng == "pe":
                    continue
                key = (eng, d.eng)
                if self.waited.get(key, -1) >= d.idx:
                    continue
                self.waited[key] = d.idx
                d.signal = True
                op.deps.append(d)
        if dma:
            n = self.dma_count[eng]
            ns = self.QUEUES[eng]
            self.dma_count[eng] = n + 1
            op.slot = (eng, n % ns)
            op.slotval = 16 * (n // ns + 1)
            prev = self.slot_last.get(op.slot)
            if prev is not None:
                op.prewait = prev
            self.slot_last[op.slot] = op
        for b in R:
            self.readers.setdefault(b, []).append(op)
        for b in W:
            self.last_w[b] = op
            self.readers[b] = []
        lst.append(op)
        return op

    def finalize(self):
        for e in self.COMPUTE:
            n = 0
            for op in self.eng_ops[e]:
                if op.signal:
                    n += 1
                    op.sigval = n

    def emit(self, eng_name, eng, sems, final_wait=False):
        for op in self.eng_ops[eng_name]:
            if op.prewait is not None:
                eng.wait_ge(sems[op.prewait.slot], op.prewait.slotval)
            need = {}
            for d in op.deps:
                if d.dma:
                    k, v = d.slot, d.slotval
                else:
                    k, v = d.eng, d.sigval
                if need.get(k, -1) < v:
                    need[k] = v
            for k, v in need.items():
                eng.wait_ge(sems[k], v)
            ins = op.fn(eng)
            if op.dma:
                ins.then_inc(sems[op.slot], 16)
            elif op.signal:
                ins.then_inc(sems[op.eng], 1)
        if final_wait:
            for slot, op in self.slot_last.items():
                if slot[0] == eng_name:
                    eng.wait_ge(sems[slot], op.slotval)


class Pool:
    def __init__(self, name, aps):
        self.name, self.aps, self.i = name, aps, 0

    def get(self):
        i = self.i % len(self.aps)
        self.i += 1
        return self.aps[i], (self.name, i)


def build_program():
    nc = bass.Bass("TRN2", target_bir_lowering=False)
    S = Sched()
    es = ExitStack()

    def dram_in(name, shape):
        return nc.dram_tensor(name, list(shape), F32, kind="ExternalInput").ap()

    def dram_out(name, shape):
        return nc.dram_tensor(name, list(shape), F32, kind="ExternalOutput").ap()

    xp_d = dram_in("xp", (512, 1024))
    xs_d = dram_in("xs", (1024, 1024))
    cond_d = dram_in("cond", (128, 16))
    params_d = dram_in("params", (128, 4 * NPL))
    consts_d = dram_in("consts", (128, NCONST))
    cmat_d = dram_in("cmat", (128, 4, 128))
    sel_d = dram_in("sel", (128, 4))
    dmask_d = dram_in("dmask", (128, 2048))
    ropeown_d = dram_in("ropeown", (128, 512))
    NG = {0: 15, 1: 14, 2: 15, 3: 14}
    wl_d = [dram_in(f"wl{l}", (NG[l], 128, 8, 512)) for l in range(DBG["layers"])]
    cbk_d = dram_in("cbk", (2, 512, 512))
    cbv_d = dram_in("cbv", (2, 512, 512))
    cck_d = dram_in("cck", (2, 512, 128))
    ccv_d = dram_in("ccv", (2, 512, 128))
    cdk_d = dram_in("cdk", (2, 512, 128))
    cdv_d = dram_in("cdv", (2, 512, 128))

    yp_d = dram_out("yp", (512, 1024))
    ys_d = dram_out("ys", (256, 1024))
    nbk_d = dram_out("nbk", (2, 2, 256, 512))
    nbv_d = dram_out("nbv", (2, 2, 256, 512))
    nck_d = dram_out("nck", (2, 2, 256, 128))
    ncv_d = dram_out("ncv", (2, 2, 256, 128))
    ndk_d = dram_out("ndk", (2, 2, 256, 128))
    ndv_d = dram_out("ndv", (2, 2, 256, 128))

    def sb(name, shape, dt):
        return es.enter_context(nc.sbuf_tensor("sb_" + name, list(shape), dt))

    TM = 1024
    xT = sb("xT", (128, 8, TM), F32)
    hT = sb("hT", (128, 8, TM), BF16)
    wring = sb("wring", (128, 3, 8, 512), BF16)
    Z1 = sb("Z1", (128, 4, TM), BF16)
    Z2 = sb("Z2", (128, 4, TM), BF16)
    Q1 = sb("Q1", (128, 4, TM), BF16)
    Q2 = sb("Q2", (128, 4, TM + 32), BF16)
    KB = sb("KB", (128, 4, 1536), BF16)
    VB = sb("VB", (128, 12, 640), BF16)
    consts = sb("consts", (128, NCONST), F32)
    params = sb("params", (128, 4 * NPL), F32)
    cb16 = sb("cb16", (128, 5, 128), BF16)
    ones16 = sb("ones16", (128, 128), BF16)
    ones32 = sb("ones32", (128, 128), F32)
    modv = sb("modv", (128, 4, 24, 2), F32)
    A1 = sb("A1", (128, 4, 8, 2), F32)
    G1 = sb("G1", (128, 4, 8, 2), F32)
    scond = sb("scond", (128, 16), BF16)
    condf = sb("condf", (128, 16), F32)
    lamt = sb("lamt", (128, 8), F32)
    esink = sb("esink", (128, 8), F32)
    cvbuf = sb("cvbuf", (128, 4, 1024), F32)
    diagb = sb("diagb", (128, 8, 128), BF16)
    stage = sb("stage", (128, 2, 1024), F32)
    ostage = sb("ostage", (128, 2, 512), F32)
    kstage = stage[:].rearrange("p a (b f) -> p (a b) f", b=2)
    ybuf = cvbuf[:].rearrange("p c (b f) -> p (c b) f", b=2)
    dmtab = stage[:, 0, :].bitcast(BF16).rearrange("p (k q) -> p k q", k=8)
    DMK = [("stage", 0, 0), ("stage", 0, 1)]
    f32t = sb("f32t", (128, 6, 512), F32)
    b16t = sb("b16t", (128, 6, 512), BF16)
    small = sb("small", (128, 8, 4), F32)
    selt = sb("selt", (128, 4), F32)
    lnneg = sb("lnneg", (128, 8), F32)
    rst = sb("rst", (128, 2, 512), F32)
    dlb = sb("dlb", (128, 1, 512), F32)
    sqb = sb("sqb", (128, 1, 512), BF16)

    ps = es.enter_context(nc.psum_tensor("ps", [128, 8, 512], F32))

    sems = {}
    for e in ("pe", "act", "dve"):
        sems[e] = es.enter_context(nc.semaphore("s_" + e))
    for q, n in Sched.QUEUES.items():
        for i in range(n):
            sems[(q, i)] = es.enter_context(nc.semaphore(f"d_{q}{i}"))

    fpool = Pool("f32t", [f32t[:, i, :] for i in range(6)])
    bpool = Pool("b16t", [b16t[:, i, :] for i in range(6)])
    rpool = Pool("rst", [rst[:, i, :] for i in range(2)])
    dlpool = Pool("dlb", [dlb[:, i, :] for i in range(1)])
    sqpool = Pool("sqb", [sqb[:, i, :] for i in range(1)])
    spool = Pool("small", [small[:, i, :] for i in range(8)])
    opool = Pool("ostage", [ostage[:, i, :] for i in range(2)])
    dpool = Pool("diag", [diagb[:, i, :] for i in range(8)])
    ring_i = [0]
    SK4 = [("stage", a, b) for a in range(2) for b in range(2)]

    def ring():
        b = ring_i[0] % 4
        ring_i[0] += 1
        return b

    def aux2():
        return ring()

    aux_i = [0]

    def aux():
        b = 4 + aux_i[0] % 4
        aux_i[0] += 1
        return b

    def mm(out, lhsT, rhs, start, stop, R, W):
        S.add("pe", lambda e: e.matmul(out, lhsT=lhsT, rhs=rhs, start=start, stop=stop), R=R, W=W)

    def tr(out, in_, R, W):
        S.add("pe", lambda e: e.transpose(out, in_, consts[:, C_ID:C_ID + 128]), R=R, W=W)

    def act(out, in_, func, R, W, scale=None, bias=None, accum=None):
        kw = {}
        if scale is not None:
            kw["scale"] = scale
        if bias is not None:
            kw["bias"] = bias
        if accum is not None:
            kw["accum_out"] = accum
        S.add("act", lambda e: e.activation(out=out, in_=in_, func=func, **kw), R=R, W=W)

    def tt(out, in0, in1, op, R, W):
        S.add("dve", lambda e: e.tensor_tensor(out=out, in0=in0, in1=in1, op=op), R=R, W=W)

    def stt(out, in0, scalar, in1, op0, op1, R, W):
        S.add("dve", lambda e: e.scalar_tensor_tensor(out=out, in0=in0, scalar=scalar, in1=in1, op0=op0, op1=op1),
              R=R, W=W)

    def ts(out, in0, s1, s2, op0, op1, R, W):
        if op1 is None:
            S.add("dve", lambda e: e.tensor_scalar(out=out, in0=in0, scalar1=s1, scalar2=None, op0=op0), R=R, W=W)
        else:
            S.add("dve", lambda e: e.tensor_scalar(out=out, in0=in0, scalar1=s1, scalar2=s2, op0=op0, op1=op1),
                  R=R, W=W)

    def vcopy(out, in_, R, W):
        S.add("dve", lambda e: e.tensor_copy(out=out, in_=in_), R=R, W=W)

    def vmemset(ap, val, W):
        S.add("dve", lambda e: e.memset(ap, val), R=(), W=W)

    def dma(q, out, in_, R, W):
        S.add(q, lambda e: e.dma_start(out=out, in_=in_), R=R, W=W, dma=True)

    cp_i = [0]

    def evac_copy(out, in_, R, W, eng=None):
        cp_i[0] += 1
        if eng == "act" or (eng is None and cp_i[0] % 2):
            act(out, in_, AF.Copy, R, W)
        else:
            vcopy(out, in_, R, W)

    def rstd_from(ps_ap, scale, R, W_out_ap, Wkey, n=512, parts=slice(0, 128)):
        tmp, tk = fpool.get()
        act(tmp[parts, :n], ps_ap, AF.Ln, R, [tk], scale=scale, bias=epsb[parts, 0:1])
        act(W_out_ap, tmp[parts, :n], AF.Exp, [tk], [Wkey], scale=-0.5)

    def run_stream(stream, L=1, extra=()):
        assert len(stream) % 2 == 0
        pairs = [(stream[i], stream[i + 1]) for i in range(0, len(stream), 2)]
        blocks = []
        cur, tot = [], 0
        for p in pairs:
            w = max(p[0]["w"], p[1]["w"])
            if cur and tot + w > 512:
                blocks.append(cur)
                cur, tot = [], 0
            cur.append(p)
            tot += w
        if cur:
            blocks.append(cur)
        state = {}
        pending = [[d, fn] for d, fn in extra]
        for bi in range(len(blocks) + L):
            if bi < len(blocks):
                if ring_i[0] % 2:
                    ring()
                banks = (ring(), ring())
                offs = []
                c0 = 0
                for p in blocks[bi]:
                    for lane in range(2):
                        p[lane]["mm"](banks[lane], c0)
                    offs.append(c0)
                    c0 += max(p[0]["w"], p[1]["w"])
                if bpool.i % 2:
                    bpool.get()
                i0 = bpool.i % len(bpool.aps)
                tiles = [bpool.get(), bpool.get()]
                act(b16t[:, i0:i0 + 2, :c0], ps[:, banks[0]:banks[0] + 2, :c0], AF.Exp,
                    [("ps", banks[0]), ("ps", banks[1])], [tiles[0][1], tiles[1][1]], scale=0.125)
                for p, o in zip(blocks[bi], offs):
                    for lane in range(2):
                        if p[lane]["post"] is not None:
                            p[lane]["post"](tiles[lane][0], tiles[lane][1], o)
                state[bi] = (tiles, offs)
            if bi >= L:
                tiles, offs = state.pop(bi - L)
                for p, o in zip(blocks[bi - L], offs):
                    for lane in range(2):
                        p[lane]["pv"](tiles[lane][0], tiles[lane][1], o)
                        for d, fn in p[lane]["fins"]:
                            pending.append([d, fn])
            keep = []
            for it in pending:
                if it[0] <= 0:
                    it[1]()
                else:
                    it[0] -= 1
                    keep.append(it)
            pending[:] = keep
        for it in pending:
            it[1]()

    wq = []
    wstate = {"n": 0, "issued": 0}

    def w_issue_upto(n):
        while wstate["issued"] < min(n, len(wq)):
            i = wstate["issued"]
            l, g = wq[i]
            slot = i % 3
            dma("pool", wring[:, slot], wl_d[l][g], R=(), W=[("w", slot)])
            wstate["issued"] += 1

    def w_next(issue=True):
        i = wstate["n"]
        wstate["n"] += 1
        if issue:
            w_issue_upto(i + 3)
        return i % 3

    epsb = sb("epsb", (128, 2), F32)
    dma("sp", consts[:], consts_d, R=(), W=["consts"])
    dma("sp", params[:], params_d, R=(), W=["params"])
    dma("sp", condf[:], cond_d, R=(), W=["condf"])
    dma("sp", selt[:], sel_d, R=(), W=["sel"])
    vmemset(epsb[:, 0:1], EPS, ["epsb"])
    vmemset(epsb[:, 1:2], 1.0, ["epsb"])
    vmemset(ones16[:], 1.0, ["ones16"])
    vmemset(ones32[:], 1.0 / 512.0, ["ones32"])
    dma("pool", cb16[:, 0:4, :], cmat_d, R=(), W=["cb16m"])
    vcopy(cb16[:, 4, :], consts[:, C_ID:C_ID + 128], ["consts"], ["cb16"])
    Rm, BD, MN, MP = cb16[:, 0, :], cb16[:, 1, :], cb16[:, 2, :], cb16[:, 3, :]
    act(scond[:], condf[:], AF.Silu, ["condf"], ["scond"])
    cosT = consts[:, C_COS:C_COS + 1024]
    sinT = consts[:, C_SIN:C_SIN + 1024]

    def pcol(l, off, n=1):
        return params[:, l * NPL + off: l * NPL + off + n]

    for pi_, g_ in enumerate(DBG["passes"]):
        for l in range(DBG["layers"]):
            if pi_ == 0 and l == 0:
                wq.extend((0, g) for g in range(6))
            wq.extend((l, g) for g in range(6, NG[l] - 2))
            if pi_ == 0 and l + 1 < DBG["layers"]:
                wq.extend((l + 1, g) for g in range(6))
            wq.extend((l, g) for g in range(NG[l] - 2, NG[l]))

    def run_pass(grp):
        isS = grp == "S"
        T = 1024 if isS else 512
        NT = T // 512
        cond_i = 1 if isS else 0
        x_d = xs_d if isS else xp_d
        y_d = ys_d if isS else yp_d
        if isS:
            units = [dict(q0=0, qn=1024, kcols=[i * 128 for i in range(12)], vblk=list(range(12)), seq=0)]
            knew0, vnew0 = 512, 4
        else:
            units = [dict(q0=u * 256, qn=256, kcols=[u * 256, u * 256 + 128], vblk=[u * 2, u * 2 + 1], seq=u)
                     for u in range(2)]
            knew0, vnew0 = 0, 0
        apad_base = (lambda col: 15 + col) if isS else (lambda col: 15 + col + 30 * (col // 256))

        def xk(c, j):
            return ("x", c, j)

        def hk(c, j):
            return ("h", c, j)

        def emit_mod(l):
            PHASE[0] = f"{grp}{l}:mod"
            MB_dummy = None
            MB = 7
            for g in range(6):
                slot = w_next()
                for n in range(4):
                    ch = g * 4 + n
                    for kc in range(8):
                        mm(ps[:, MB, ch * 2:ch * 2 + 2], wring[:, slot, kc, n * 128:(n + 1) * 128],
                           scond[:, kc * 2:kc * 2 + 2], kc == 0, kc == 7,
                           R=[("w", slot), "scond"], W=[("ps", MB)])
            for k in range(2):
                tt(modv[:, l, :, k], ps[:, MB, 0:48].rearrange("p (c k) -> p c k", k=2)[:, :, k],
                   pcol(l, P_BMOD, 24), ALU.add, R=[("ps", MB), "params"], W=[("modv", l, k)])
                stt(A1[:, l, :, k], modv[:, l, 8:16, k], 1.0, pcol(l, P_NPRE, 8), ALU.add, ALU.mult,
                    R=[("modv", l, k), "params"], W=[("A1", l, k)])
                tt(G1[:, l, :, k], modv[:, l, 16:24, k], pcol(l, P_NPOST, 8), ALU.mult,
                   R=[("modv", l, k), "params"], W=[("G1", l, k)])

        def mod_closures(ln):
            out = []

            def modg(g):
                slot = w_next()
                b = ring()
                for n in range(4):
                    for kc in range(8):
                        mm(ps[:, b, n * 2:n * 2 + 2], wring[:, slot, kc, n * 128:(n + 1) * 128],
                           scond[:, kc * 2:kc * 2 + 2], kc == 0, kc == 7, R=[("w", slot), "scond"], W=[("ps", b)])
                for k in range(2):
                    tt(modv[:, ln, g * 4:(g + 1) * 4, k], ps[:, b, 0:8].rearrange("p (c k) -> p c k", k=2)[:, :, k],
                       pcol(ln, P_BMOD + g * 4, 4), ALU.add, R=[("ps", b), "params"], W=[("modvp", ln, g, k)])

            def modfin():
                for k in range(2):
                    allp = [("modvp", ln, g, k) for g in range(6)]
                    stt(A1[:, ln, :, k], modv[:, ln, 8:16, k], 1.0, pcol(ln, P_NPRE, 8), ALU.add, ALU.mult,
                        R=allp + ["params"], W=[("A1", ln, k), ("modv", ln, k)])
                    tt(G1[:, ln, :, k], modv[:, ln, 16:24, k], pcol(ln, P_NPOST, 8), ALU.mult,
                       R=allp + ["params"], W=[("G1", ln, k)])

            for g in range(6):
                out.append((1 + 3 * g, (lambda g=g: modg(g))))
            out.append((17, modfin))
            return out

        first_mod_done = [False]
        PHASE[0] = f"{grp}:prologue"
        def prologue_blocks(xd, blks):
            for blk in blks:
                st = blk % 2
                dma("sp", stage[:, st, :], xd[blk * 128:(blk + 1) * 128, :], R=(), W=[("stage", st, 0), ("stage", st, 1)])
                for half in range(2):
                    b = ring()
                    for q in range(4):
                        fc = half * 4 + q
                        tr(ps[:, b, q * 128:(q + 1) * 128], stage[:, st, fc * 128:(fc + 1) * 128],
                           R=[("stage", st, 0), ("stage", st, 1), "consts"], W=[("ps", b)])
                    evac_copy(xT[:, half * 4:half * 4 + 4, blk * 128:(blk + 1) * 128],
                              ps[:, b, :].rearrange("p (q t) -> p q t", q=4),
                              R=[("ps", b)], W=[xk(half * 4 + q, blk // 4) for q in range(4)], eng="act")

        hoist = DBG["passes"] == ("P", "S")
        nblk0 = (4 if hoist else 8) if isS else 4
        if grp == DBG["passes"][0] and DBG["layers"] > 0:
            cl0 = mod_closures(0)
            for i_, (_d, fn_) in enumerate(cl0[:6]):
                fn_()
                if i_ < nblk0:
                    prologue_blocks(x_d, [i_])
            prologue_blocks(x_d, range(6, nblk0))
            cl0[6][1]()
        else:
            prologue_blocks(x_d, range(nblk0))

        for l in range(DBG["layers"]):
            even = l % 2 == 0
            li = l // 2
            last = isS and l == 3 and DBG["layers"] == 4
            next_mod = mod_closures(l + 1) if (grp == DBG["passes"][0] and l + 1 < DBG["layers"]) else []
            lam_init = 0.8 - 0.6 * math.exp(-0.3 * l)

            if DBG.get("stop") == "mod":
                return
            ck = cond_i
            modR = [("modv", l, ck), ("A1", l, ck), ("G1", l, ck)]

            PHASE[0] = f"{grp}{l}:prenorm"
            SB_ = 6
            for j in range(NT):
                cs = slice(j * 512, (j + 1) * 512)
                for c in range(8):
                    sq, sk = bpool.get()
                    act(sq, xT[:, c, cs], AF.Square, [xk(c, j)], [sk])
                    mm(ps[:, SB_, :], ones16[:], sq, c == 0, c == 7, R=[sk, "ones16"], W=[("ps", SB_)])
                rs, rk = rpool.get()
                rstd_from(ps[:, SB_, :], 1.0 / 1024.0, [("ps", SB_), "epsb"], rs, rk)
                for c in range(8):
                    tmp, tk = fpool.get()
                    stt(tmp, xT[:, c, cs], A1[:, l, c, ck:ck + 1], rs, ALU.mult, ALU.mult,
                        R=[xk(c, j), rk] + modR, W=[tk])
                    act(hT[:, c, cs], tmp, AF.Identity, [tk] + modR, [hk(c, j)], bias=modv[:, l, c, ck:ck + 1])

            if DBG.get("stop") == "prenorm":
                return

            pipe = []

            def pipe_tick():
                for g_ in list(pipe):
                    try:
                        next(g_)
                    except StopIteration:
                        pipe.remove(g_)

            def pipe_flush():
                while pipe:
                    pipe_tick()

            def pipe_add(gen):
                try:
                    next(gen)
                    pipe.append(gen)
                except StopIteration:
                    pass

            def proj_fm(slot, n, j):
                b = ring()
                for kc in range(8):
                    mm(ps[:, b, :], wring[:, slot, kc, n * 128:(n + 1) * 128], hT[:, kc, j * 512:(j + 1) * 512],
                       kc == 0, kc == 7, R=[("w", slot), hk(kc, j)], W=[("ps", b)])
                pipe_tick()
                return b

            def proj_tm(slot, blk, ncols=512):
                b = ring()
                for kc in range(8):
                    mm(ps[:, b, :ncols], hT[:, kc, blk * 128:(blk + 1) * 128], wring[:, slot, kc, :ncols],
                       kc == 0, kc == 7, R=[("w", slot), hk(kc, blk // 4)], W=[("ps", b)])
                pipe_tick()
                return b

            def rope_gen(dst, dkey, src_ps_or_sb, srcR, j, raw_ready=None, N=512, tabs=None):
                if tabs is None:
                    cos_, sin_, tR = cosT[:, j * 512:(j + 1) * 512], sinT[:, j * 512:(j + 1) * 512], ["consts"]
                else:
                    cos_, sin_, tR = tabs
                if raw_ready is None:
                    raw, rk_ = bpool.get()
                    evac_copy(raw[:, :N], src_ps_or_sb, srcR, [rk_])
                    yield
                    yield
                else:
                    raw, rk_ = raw_ready
                b2 = aux()
                mm(ps[:, b2, :N], Rm, raw[:, :N], True, True, R=[rk_, "cb16", "cb16m"], W=[("ps", b2)])
                t1, k1 = fpool.get()
                tt(t1[:, :N], raw[:, :N], cos_, ALU.mult, [rk_] + tR, [k1])
                t2, k2 = fpool.get()
                tt(t2[:, :N], ps[:, b2, :N], sin_, ALU.mult, [("ps", b2)] + tR, [k2])
                tt(dst, t1[:, :N], t2[:, :N], ALU.add, [k1, k2], [dkey])

            def rope_to(dst, dkey, src_ps_or_sb, srcR, j, from_psum):
                pipe_add(rope_gen(dst, dkey, src_ps_or_sb, srcR, j))

            if even:
                vmemset(Q2[:, :, :], 0.0, [("q2", c, j) for c in range(4) for j in range(NT)] + ["q2pad"])
                lv = pcol(l, P_LAM, 256)
                for i2 in range(2):
                    pr, pk = fpool.get()
                    tt(pr[:, 0:64], lv[:, i2 * 128:i2 * 128 + 64], lv[:, i2 * 128 + 64:i2 * 128 + 128], ALU.mult,
                       ["params"], [pk])
                    act(pr[:, 64:128], pr[:, 0:64], AF.Identity, [pk], [pk, ("lamt", i2)], accum=lamt[:, i2:i2 + 1])
                    act(lamt[:, 2 + i2:3 + i2], lamt[:, i2:i2 + 1], AF.Exp, [("lamt", i2)], [("lamt", 2 + i2)])
                stt(lamt[:, 4:5], lamt[:, 2:3], -1.0, lamt[:, 3:4], ALU.mult, ALU.add,
                    [("lamt", 2), ("lamt", 3)], [("lamt", 4)])
                ts(lamt[:, 4:5], lamt[:, 4:5], -lam_init, None, ALU.add, None, [("lamt", 4)], [("lamt", 4)])
                ts(lamt[:, 5:6], pcol(l, P_SUBG, 1), 1.0 - lam_init, None, ALU.mult, None, ["params"], [("lamt", 5)])
                nlam = lamt[:, 4:5]
                sgs = lamt[:, 5:6]

                if DBG.get("stop") == "lam":
                    return
                PHASE[0] = f"{grp}{l}:inproj"
                for g in range(7):
                    if DBG.get("stop") == "g%d" % g:
                        return
                    slot = w_next()
                    if g < 2:
                        for pair in range(2):
                            c = g * 2 + pair
                            for j in range(NT):
                                bg = proj_fm(slot, pair * 2, j)
                                sg_, sgk = fpool.get()
                                act(sg_, ps[:, bg, :], AF.Sigmoid, [("ps", bg)], [sgk])
                                bu = proj_fm(slot, pair * 2 + 1, j)
                                if isS:
                                    tt(Q2[:, c, 15 + j * 512:15 + (j + 1) * 512], ps[:, bu, :], sg_, ALU.mult,
                                       [("ps", bu), sgk], [("q2", c, j)])
                                else:
                                    tt(Q2[:, c, 0:572].rearrange("p (s t) -> p s t", s=2)[:, :, 15:271],
                                       ps[:, bu, :].rearrange("p (s t) -> p s t", s=2),
                                       sg_.rearrange("p (s t) -> p s t", s=2), ALU.mult,
                                       [("ps", bu), sgk], [("q2", c, j)])
                    elif g in (2, 3):
                        Z = Z1 if g == 2 else Z2
                        zn = "z1" if g == 2 else "z2"
                        for n in range(4):
                            for j in range(NT):
                                b = proj_fm(slot, n, j)
                                act(Z[:, n, j * 512:(j + 1) * 512], ps[:, b, :], AF.Silu, [("ps", b)], [(zn, n, j)])
                    elif g == 4:
                        for n in range(4):
                            for j in range(NT):
                                b = proj_fm(slot, n, j)
                                if isS:
                                    rope_to(Q1[:, n, j * 512:(j + 1) * 512], ("q1", n, j), ps[:, b, :], [("ps", b)], j, True)
                                else:
                                    evac_copy(Q1[:, n, j * 512:(j + 1) * 512], ps[:, b, :], [("ps", b)], [("q1", n, j)])
                    elif g == 5:
                        for n in range(4):
                            for j in range(NT):
                                b = proj_fm(slot, n, j)
                                dst = KB[:, n, knew0 + j * 512:knew0 + (j + 1) * 512]
                                if isS:
                                    rope_to(dst, ("k", n, 1 + j), ps[:, b, :], [("ps", b)], j, True)
                                else:
                                    evac_copy(dst, ps[:, b, :], [("ps", b)], [("k", n, j)])
                        if not isS:
                            for blk in list(range(4)) * DBG.get("rep5", 1):
                                b = proj_tm(slot, blk)
                                o_, ok = opool.get()
                                evac_copy(o_, ps[:, b, :], [("ps", b)], [ok])
                                dma("sp", nbk_d[blk // 2, li, (blk % 2) * 128:(blk % 2 + 1) * 128, :], o_, [ok], [])
                    else:
                        for blk in range(T // 128):
                            b = proj_tm(slot, blk)
                            if not DBG.get("skipvb"):
                                evac_copy(VB[:, vnew0 + blk, 0:512], ps[:, b, :], [("ps", b)], [("v", vnew0 + blk)])
                            if not isS and not DBG.get("skipnbv"):
                                o_, ok = opool.get()
                                evac_copy(o_, ps[:, b, :], [("ps", b)], [ok])
                                dma("sp", nbv_d[blk // 2, li, (blk % 2) * 128:(blk % 2 + 1) * 128, :], o_, [ok], [])
                pipe_flush()
                if isS:
                    dma("sp", kstage[:], cbk_d[li].rearrange("(b p) f -> p b f", p=128), R=(), W=SK4)
                    for c in range(4):
                        b = ring()
                        for blk in range(4):
                            tr(ps[:, b, blk * 128:(blk + 1) * 128], kstage[:, blk, c * 128:(c + 1) * 128],
                               R=SK4 + ["consts"], W=[("ps", b)])
                        evac_copy(KB[:, c, 0:512], ps[:, b, :], [("ps", b)], [("k", c, 0)])
                    dma("pool", VB[:, 0:4, 0:512], cbv_d[li].rearrange("(b p) f -> p b f", p=128), R=(),
                        W=[("v", i) for i in range(4)])

                if DBG.get("stop") == "inproj":
                    return
                PHASE[0] = f"{grp}{l}:conv"
                tiles = []
                if isS:
                    tiles = [(0, 512), (512, 512)]
                else:
                    tiles = [(0, 256), (256, 256)]
                for c in range(4):
                    banks = [4 + (c % 2) * 2, 5 + (c % 2) * 2]
                    for k in range(CONV_K):
                        dg, dk_ = dpool.get()
                        ts(dg, cb16[:, 4, :], pcol(l, P_CONVW + c * 31 + k, 1), None, ALU.mult, None,
                           ["cb16", "cb16m", "params"], [dk_])
                        for ti, (c0, N) in enumerate(tiles):
                            a0 = apad_base(c0) - 15 + k
                            mm(ps[:, banks[ti], :N], dg, Q2[:, c, a0:a0 + N], k == 0, k == CONV_K - 1,
                               R=[dk_, "q2pad"] + [("q2", c, j) for j in range(NT)], W=[("ps", banks[ti])])
                    for ti, (c0, N) in enumerate(tiles):
                        act(cvbuf[:, c, c0:c0 + N], ps[:, banks[ti], :N], AF.Identity, [("ps", banks[ti]), "params"],
                            [("ycv", 2 * c + c0 // 512)], bias=pcol(l, P_CONVB + c, 1))
                ts(lnneg[:, 0:8], pcol(l, P_LNG, 8), -1.0, None, ALU.mult, None, ["params"], [("lnneg", l)])
                ln_extra = []
                for ti, (c0, N) in enumerate(tiles):
                    lst = {}

                    def ln_stats(lst=lst, c0=c0, N=N):
                        bA, bB = ring(), ring()
                        for c in range(4):
                            mm(ps[:, bA, :N], ones32[:], cvbuf[:, c, c0:c0 + N], c == 0, c == 3,
                               R=[("ycv", 2 * c + c0 // 512), "ones32"], W=[("ps", bA)])
                        for c in range(4):
                            sq, sk = fpool.get()
                            act(sq[:, :N], cvbuf[:, c, c0:c0 + N], AF.Square, [("ycv", 2 * c + c0 // 512)], [sk])
                            mm(ps[:, bB, :N], ones32[:], sq[:, :N], c == 0, c == 3, R=[sk, "ones32"], W=[("ps", bB)])
                        lst["bA"], lst["bB"] = bA, bB

                    def ln_rstd(lst=lst, N=N):
                        bA, bB = lst["bA"], lst["bB"]
                        mean, mnk = rpool.get()
                        act(mean[:, :N], ps[:, bA, :N], AF.Copy, [("ps", bA)], [mnk])
                        m2, mk = fpool.get()
                        act(m2[:, :N], ps[:, bA, :N], AF.Square, [("ps", bA)], [mk])
                        var, vk = fpool.get()
                        tt(var[:, :N], ps[:, bB, :N], m2[:, :N], ALU.subtract, [("ps", bB), mk], [vk])
                        rs, rk = rpool.get()
                        rstd_from(var[:, :N], 1.0, [vk, "epsb"], rs[:, :N], rk, n=N)
                        lst.update(mean=mean, mnk=mnk, rs=rs, rk=rk)

                    def ln_chunk(c, lst=lst, c0=c0, N=N):
                        mean, mnk, rs, rk = lst["mean"], lst["mnk"], lst["rs"], lst["rk"]
                        t1, k1 = fpool.get()
                        tt(t1[:, :N], cvbuf[:, c, c0:c0 + N], mean[:, :N], ALU.subtract,
                           [("ycv", 2 * c + c0 // 512), mnk], [k1])
                        tt(t1[:, :N], t1[:, :N], rs[:, :N], ALU.mult, [k1, rk], [k1])
                        t2, k2 = fpool.get()
                        act(t2[:, :N], t1[:, :N], AF.Exp, [k1, ("lnneg", l)], [k2], scale=lnneg[:, c:c + 1],
                            bias=lnneg[:, 4 + c:5 + c])
                        act(t2[:, :N], t2[:, :N], AF.Ln, [k2, "epsb"], [k2], bias=epsb[:, 1:2])
                        act(t2[:, :N], t2[:, :N], AF.Exp, [k2], [k2], scale=-1.0)
                        ts(t1[:, :N], t1[:, :N], pcol(l, P_LNG + c, 1), pcol(l, P_LNB + c, 1), ALU.mult, ALU.add,
                           [k1, "params"], [k1])
                        tt(t1[:, :N], t1[:, :N], t2[:, :N], ALU.mult, [k1, k2], [k1])
                        tt(hT[:, c, c0:c0 + N], t1[:, :N], Z1[:, c, c0:c0 + N], ALU.mult,
                           [k1, ("z1", c, c0 // 512)], [hk(c, c0 // 512)])

                    base = ti * 6
                    ln_extra.append((base, ln_stats))
                    ln_extra.append((base + 1, ln_rstd))
                    for c in range(4):
                        ln_extra.append((base + 2 + c, (lambda c=c, f=ln_chunk: f(c))))

                if DBG.get("stop") == "apath":
                    return
                PHASE[0] = f"{grp}{l}:attn"
                stream = []
                OB = (4, 5)
                DB = (6, 7)
                QN = 256
                subtiles = [(u, qt) for u in units for qt in range(u["qn"] // QN)]
                for g0 in range(0, len(subtiles), 2):
                    grp_ = subtiles[g0:g0 + 2]
                    gq0 = grp_[0][0]["q0"] + grp_[0][1] * QN
                    gs = slice(gq0, gq0 + 512)
                    jq = gq0 // 512
                    for h in range(4):
                        for sub, (u, qt) in enumerate(grp_):
                            q0 = u["q0"] + qt * QN
                            qs = slice(q0, q0 + QN)
                            a0 = sub * QN
                            nkb = len(u["kcols"])
                            for kbi, kc0 in enumerate(u["kcols"]):
                                kkey = ("k", h, (0 if kc0 < 512 else 1 + (kc0 - 512) // 512)) if isS else ("k", h, 0)
                                vb_ = u["vblk"][kbi]
                                for cc in range(2):
                                    def mm_(b, c0, kc0=kc0, cc=cc, h=h, qs=qs, kkey=kkey, jq=jq):
                                        pr_ = slice(cc * 64, cc * 64 + 64)
                                        mm(ps[:, b, c0:c0 + QN], KB[pr_, h, kc0:kc0 + 128], Q1[pr_, h, qs], True, True,
                                           R=[kkey, ("q1", h, jq)], W=[("ps", b)])

                                    def pv(pT, pk, c0, cc=cc, h=h, vb_=vb_, a0=a0, first=(kbi == 0), last=(kbi == nkb - 1)):
                                        mm(ps[:, OB[cc], a0:a0 + QN], VB[:, vb_, h * 128:(h + 1) * 128], pT[:, c0:c0 + QN],
                                           first, last, R=[pk, ("v", vb_)], W=[("ps", OB[cc])])
                                        mm(ps[:, DB[cc], a0:a0 + QN], ones16[:], pT[:, c0:c0 + QN], first, last,
                                           R=[pk, "ones16"], W=[("ps", DB[cc])])

                                    fins = []
                                    if sub == len(grp_) - 1 and kbi == nkb - 1 and cc == 1:
                                        fst = {}
                                        FN = 512

                                        def fin1(fst=fst, FN=FN):
                                            tl = []
                                            for c2 in range(2):
                                                ln_, lk = fpool.get()
                                                act(ln_[:, :FN], ps[:, DB[c2], :FN], AF.Ln, [("ps", DB[c2])], [lk])
                                                act(ln_[:, :FN], ln_[:, :FN], AF.Exp, [lk], [lk], scale=-1.0)
                                                t_, tk_ = fpool.get()
                                                tt(t_[:, :FN], ps[:, OB[c2], :FN], ln_[:, :FN], ALU.mult,
                                                   [("ps", OB[c2]), lk], [tk_])
                                                tl.append((t_, tk_))
                                            dl, dk2 = dlpool.get()
                                            stt(dl[:, :FN], tl[1][0][:, :FN], nlam, tl[0][0][:, :FN], ALU.mult, ALU.add,
                                                [tl[0][1], tl[1][1], ("lamt", 4)], [dk2])
                                            sq, sk = sqpool.get()
                                            act(sq[:, :FN], dl[:, :FN], AF.Square, [dk2], [sk])
                                            fst.update(dl=dl, dk2=dk2, sq=sq, sk=sk)

                                        def fin2(fst=fst, h=h, gs=gs, jq=jq, FN=FN):
                                            dl, dk2, sq, sk = fst["dl"], fst["dk2"], fst["sq"], fst["sk"]
                                            b = ring()
                                            ring()
                                            mm(ps[:, b, :FN], ones16[:], sq[:, :FN], True, True, R=[sk, "ones16"],
                                               W=[("ps", b)])
                                            rs, rk = fpool.get()
                                            rstd_from(ps[:, b, :FN], 1.0 / 128.0, [("ps", b), "epsb"], rs[:, :FN], rk, n=FN)
                                            stt(dl[:, :FN], dl[:, :FN], sgs, rs[:, :FN], ALU.mult, ALU.mult,
                                                [dk2, rk, ("lamt", 5)], [dk2])
                                            tt(hT[:, 4 + h, gs], dl[:, :FN], Z2[:, h, gs], ALU.mult, [dk2, ("z2", h, jq)],
                                               [hk(4 + h, jq)])

                                        fins = [(0, fin1), (1, fin2)]
                                    stream.append(dict(w=QN, mm=mm_, post=None, pv=pv, fins=fins))
                run_stream(stream, 1, extra=ln_extra + next_mod)
            else:
                act(esink[:], pcol(l, P_SINK, 8), AF.Exp, ["params"], ["esink"])
                for reg in range(2):
                    vmemset(VB[:, :, reg * 320:reg * 320 + 320].rearrange("p b (t e) -> p b t e", e=64)[:, :, 0::2, :],
                            1.0, ["vones"] + [("v", i) for i in range(12)])

                def headnorm_gen(dst, dkey, b, gain, j, N=512, tabs=None):
                    sq, sk = bpool.get()
                    act(sq[:, :N], ps[:, b, :N], AF.Square, [("ps", b)], [sk])
                    yield
                    b2 = aux()
                    mm(ps[:, b2, :N], BD, sq[:, :N], True, True, R=[sk, "cb16", "cb16m"], W=[("ps", b2)])
                    rs, rk = fpool.get()
                    rstd_from(ps[:, b2, :N], 1.0, [("ps", b2), "epsb"], rs[:, :N], rk, n=N)
                    if isS:
                        qn_, qk = bpool.get()
                        stt(qn_[:, :N], ps[:, b, :N], gain, rs[:, :N], ALU.mult, ALU.mult, [("ps", b), rk, "params"], [qk])
                        yield
                        yield
                        yield from rope_gen(dst, dkey, None, None, j, raw_ready=(qn_, qk), N=N, tabs=tabs)
                    else:
                        stt(dst, ps[:, b, :N], gain, rs[:, :N], ALU.mult, ALU.mult, [("ps", b), rk, "params"], [dkey])

                def headnorm_to(dst, dkey, b, gain, j):
                    pipe_add(headnorm_gen(dst, dkey, b, gain, j))

                PHASE[0] = f"{grp}{l}:inproj"
                def tm_block(slot, blk):
                    ncols = 256 if isS else 512
                    b = proj_tm(slot, blk, ncols)
                    for reg in range(2):
                        evac_copy(VB[:, vnew0 + blk, reg * 320:reg * 320 + 320]
                                  .rearrange("p (t e) -> p t e", e=64)[:, 1::2, :],
                                  ps[:, b, reg * 128:reg * 128 + 128].rearrange("p (t e) -> p t e", e=64),
                                  [("ps", b), "vones"], [("v", vnew0 + blk, reg)])
                    if not isS:
                        o_, ok = opool.get()
                        evac_copy(o_, ps[:, b, :], [("ps", b)], [ok])
                        sq_, bi = blk // 2, (blk % 2) * 128
                        dma("sp", ncv_d[sq_, li, bi:bi + 128, :], o_[:, 0:128], [ok], [])
                        dma("sp", ndv_d[sq_, li, bi:bi + 128, :], o_[:, 128:256], [ok], [])
                        dma("sp", ndk_d[sq_, li, bi:bi + 128, :], o_[:, 384:512], [ok], [])
                        o2, ok2 = opool.get()
                        sm, smk = spool.get()
                        for hh in range(2):
                            act(o2[:, hh * 64:(hh + 1) * 64], o_[:, 256 + hh * 64:256 + (hh + 1) * 64], AF.Square,
                                [ok], [ok2, smk], accum=sm[:, hh:hh + 1])
                        act(sm[:, 0:2], sm[:, 0:2], AF.Ln, [smk, "epsb"], [smk], scale=1.0 / 64.0, bias=epsb[:, 0:1])
                        act(sm[:, 0:2], sm[:, 0:2], AF.Exp, [smk], [smk], scale=-0.5)
                        for hh in range(2):
                            stt(o2[:, hh * 64:(hh + 1) * 64], o_[:, 256 + hh * 64:256 + (hh + 1) * 64],
                                sm[:, hh:hh + 1], pcol(l, P_KNF, 64), ALU.mult, ALU.mult,
                                [ok, smk, "params"], [ok2])
                        dma("sp", nck_d[sq_, li, bi:bi + 128, :], o2[:, 0:128], [ok2], [])

                if last:
                    h_own = cvbuf[:, 0, :].bitcast(BF16).rearrange("p (c t) -> p c t", c=8)
                    HOK = [("ycv", 0), ("ycv", 1)]
                    for kc in range(8):
                        dst = h_own[:, kc, :]
                        ts(dst, hT[:, kc, 0:256], selt[:, 0:1], None, ALU.mult, None, [hk(kc, 0), "sel"], HOK)
                        for r_ in range(1, 4):
                            stt(dst, hT[:, kc, r_ * 256:(r_ + 1) * 256], selt[:, r_:r_ + 1], dst, ALU.mult, ALU.add,
                                [hk(kc, r_ // 2), "sel"] + HOK, HOK)
                    dma("sp", stage[:, 1, 0:512], ropeown_d, R=(), W=[("stage", 1, 0)])
                    own_tabs = (stage[:, 1, 0:256], stage[:, 1, 256:512], [("stage", 1, 0)])

                    def proj_own(slot, n):
                        b = ring()
                        for kc in range(8):
                            mm(ps[:, b, :256], wring[:, slot, kc, n * 128:(n + 1) * 128], h_own[:, kc, :],
                               kc == 0, kc == 7, R=[("w", slot)] + HOK, W=[("ps", b)])
                        pipe_tick()
                        return b

                for g in range(4):
                    slot = w_next()
                    Z = Z1 if g < 2 else Z2
                    zn = "z1" if g < 2 else "z2"
                    for pair in range(2):
                        c = (g % 2) * 2 + pair
                        if last:
                            b = proj_own(slot, pair * 2)
                            if g < 2:
                                pipe_add(headnorm_gen(Q1[:, c, 0:256], ("q1", c, 0), b, pcol(l, P_QN, 1), 0, N=256,
                                                      tabs=own_tabs))
                            else:
                                pipe_add(rope_gen(Q2[:, c, 0:256], ("q2", c, 0), ps[:, b, :256], [("ps", b)], 0, N=256,
                                                  tabs=own_tabs))
                            continue
                        for j in range(NT):
                            b = proj_fm(slot, pair * 2, j)
                            if g < 2:
                                headnorm_to(Q1[:, c, j * 512:(j + 1) * 512], ("q1", c, j), b, pcol(l, P_QN, 1), j)
                            else:
                                dst = Q2[:, c, j * 512:(j + 1) * 512]
                                if isS:
                                    rope_to(dst, ("q2", c, j), ps[:, b, :], [("ps", b)], j, True)
                                else:
                                    evac_copy(dst, ps[:, b, :], [("ps", b)], [("q2", c, j)])
                    for pair in range(2):
                        c = (g % 2) * 2 + pair
                        if last:
                            b = proj_own(slot, pair * 2 + 1)
                            act(Z[:, c, 0:256], ps[:, b, :256], AF.Silu, [("ps", b)], [(zn, c, 0)])
                            continue
                        for j in range(NT):
                            b = proj_fm(slot, pair * 2 + 1, j)
                            act(Z[:, c, j * 512:(j + 1) * 512], ps[:, b, :], AF.Silu, [("ps", b)], [(zn, c, j)])
                slot = w_next()
                slot5 = w_next(issue=False)
                nblk = T // 128
                for n in range(4):
                    for j in range(NT):
                        b = proj_fm(slot, n, j)
                        dst = KB[:, n, knew0 + j * 512:knew0 + (j + 1) * 512]
                        dkey = ("k", n, 1 + j) if isS else ("k", n, j)
                        if n < 2:
                            headnorm_to(dst, dkey, b, pcol(l, P_KN, 1), j)
                        elif isS:
                            rope_to(dst, dkey, ps[:, b, :], [("ps", b)], j, True)
                        else:
                            evac_copy(dst, ps[:, b, :], [("ps", b)], [dkey])
                    for blk in range(n * nblk // 4, (n + 1) * nblk // 4):
                        tm_block(slot5, blk)
                pipe_flush()
                if isS:
                    for mi, (ck_d_, cv_d_) in enumerate(((cck_d, ccv_d), (cdk_d, cdv_d))):
                        for kv in range(2):
                            for dup in range(2):
                                dma("sp", kstage[:, :, mi * 256 + kv * 128 + dup * 64: mi * 256 + kv * 128 + dup * 64 + 64],
                                    ck_d_[li].rearrange("(b p) f -> p b f", p=128)[:, :, kv * 64:(kv + 1) * 64],
                                    R=(), W=SK4)
                        for kv in range(2):
                            b = ring()
                            for blk in range(4):
                                tr(ps[:, b, blk * 128:(blk + 1) * 128],
                                   kstage[:, blk, mi * 256 + kv * 128: mi * 256 + kv * 128 + 128],
                                   R=SK4 + ["consts"], W=[("ps", b)])
                            evac_copy(KB[:, mi * 2 + kv, 0:512], ps[:, b, :], [("ps", b)], [("k", mi * 2 + kv, 0)])
                        for kv in range(2):
                            dma("pool", VB[:, 0:4, mi * 320 + 64 + kv * 128:mi * 320 + 128 + kv * 128],
                                cv_d_[li].rearrange("(b p) f -> p b f", p=128)[:, :, kv * 64:(kv + 1) * 64], R=["vones"],
                                W=[("v", i, mi, kv) for i in range(4)])

                PHASE[0] = f"{grp}{l}:attn"
                att_units = units
                if last:
                    def blend(buf, c, key):
                        dst = buf[:, c, 0:256]
                        ts(dst, buf[:, c, 0:256], selt[:, 0:1], None, ALU.mult, None, [key(c, 0), "sel"], [key(c, 0)])
                        for r_ in range(1, 4):
                            stt(dst, buf[:, c, r_ * 256:(r_ + 1) * 256], selt[:, r_:r_ + 1], dst, ALU.mult, ALU.add,
                                [key(c, r_ // 2), key(c, 0), "sel"], [key(c, 0)])
                    for c in range(8):
                        blend(xT, c, lambda c_, j_: xk(c_, j_))
                    for hf in range(2):
                        dma("pool", stage[:, 0, hf * 512:(hf + 1) * 512].bitcast(BF16), dmask_d[:, hf * 1024:(hf + 1) * 1024],
                            R=(), W=[("stage", 0, hf)])
                    att_units = [dict(q0=0, qn=256, kcols=[i * 128 for i in range(12)], vblk=list(range(12)), seq=0)]
                acc_i = [0]
                stream = []
                QN = 256
                subtiles = [(u, qt) for u in att_units for qt in range(u["qn"] // QN)]
                for mi in range(2):
                    Qb = Q1 if mi == 0 else Q2
                    qn_ = "q1" if mi == 0 else "q2"
                    Zb = Z1 if mi == 0 else Z2
                    zn = "z1" if mi == 0 else "z2"
                    for g0 in range(0, len(subtiles), 2):
                        grp_ = subtiles[g0:g0 + 2]
                        gq0 = grp_[0][0]["q0"] + grp_[0][1] * QN
                        GW = QN * len(grp_)
                        gs = slice(gq0, gq0 + GW)
                        jq = gq0 // 512
                        for c in range(4):
                            ABs = []
                            for half in range(2):
                                ABs.append(4 + acc_i[0] % 4)
                                acc_i[0] += 1
                            for sub, (u, qt) in enumerate(grp_):
                                q0 = u["q0"] + qt * QN
                                a0 = sub * QN
                                steps = []
                                if mi == 1 and last:
                                    for kbi in range(4):
                                        steps.append(([(kbi * 128, kbi, None)], 0, QN))
                                    for kb in range(8):
                                        steps.append(([(512 + kb * 128, 4 + kb, dmtab[:, kb, :])], 0, QN))
                                elif mi == 1 and isS:
                                    for kbi in range(4):
                                        steps.append(([(kbi * 128, kbi, None)], 0, QN))
                                    for qb in range(QN // 128):
                                        gq = qt * (QN // 128) + qb
                                        subs = []
                                        for kb in (gq - 1, gq, gq + 1):
                                            if 0 <= kb < 8:
                                                msk = None if kb == gq else (MN if kb == gq + 1 else MP)
                                                subs.append((512 + kb * 128, 4 + kb, msk))
                                        steps.append((subs, qb * 128, 128))
                                else:
                                    for kbi, kc0 in enumerate(u["kcols"]):
                                        steps.append(([(kc0, u["vblk"][kbi], None)], 0, QN))
                                for si, (subs, s0, sn) in enumerate(steps):
                                    for half in range(2):
                                        h = c * 2 + half
                                        kv = h // 4
                                        kchunk = mi * 2 + kv
                                        AB = ABs[half]
                                        vc0 = mi * 320 + (64 + 128 * kv if half == 0 else 128 * kv)

                                        def mm_(b, c0, half=half, kchunk=kchunk, subs=subs, Qb=Qb, c=c, q0=q0, s0=s0, sn=sn,
                                                qn_=qn_, jq=jq):
                                            pr_ = slice(half * 64, half * 64 + 64)
                                            for i_, (kc0, vb, msk) in enumerate(subs):
                                                kkey = ("k", kchunk, (0 if kc0 < 512 else 1 + (kc0 - 512) // 512)) if isS \
                                                    else ("k", kchunk, 0)
                                                mm(ps[:, b, c0 + i_ * sn:c0 + (i_ + 1) * sn], KB[pr_, kchunk, kc0:kc0 + 128],
                                                   Qb[pr_, c, q0 + s0:q0 + s0 + sn], True, True,
                                                   R=[kkey, (qn_, c, jq)], W=[("ps", b)])

                                        post = None
                                        if any(m is not None for (_, _, m) in subs):
                                            def post(pT, pk, c0, subs=subs, sn=sn):
                                                for i_, (kc0, vb, msk) in enumerate(subs):
                                                    if msk is not None:
                                                        tt(pT[:, c0 + i_ * sn:c0 + (i_ + 1) * sn],
                                                           pT[:, c0 + i_ * sn:c0 + (i_ + 1) * sn], msk, ALU.mult,
                                                           [pk, "cb16", "cb16m"] + DMK, [pk])

                                        def pv(pT, pk, c0, AB=AB, s0=s0, sn=sn, subs=subs, vc0=vc0, mi=mi, kv=kv, a0=a0,
                                               first=(si == 0), last=(si == len(steps) - 1)):
                                            for i_, (kc0, vb, msk) in enumerate(subs):
                                                mm(ps[:, AB, a0 + s0:a0 + s0 + sn], VB[:, vb, vc0:vc0 + 128],
                                                   pT[:, c0 + i_ * sn:c0 + (i_ + 1) * sn],
                                                   first and i_ == 0, last and i_ == len(subs) - 1,
                                                   R=[pk, ("v", vb, mi), ("v", vb, mi, kv), ("v", vb), "vones"], W=[("ps", AB)])

                                        fins = []
                                        if sub == len(grp_) - 1 and si == len(steps) - 1:
                                            def fin(half=half, AB=AB, mi=mi, h=h, Zb=Zb, zn=zn, c=c, gs=gs, jq=jq, GW=GW):
                                                FN = GW
                                                pr_ = slice(half * 64, half * 64 + 64)
                                                dr = slice(64, 128) if half == 0 else slice(0, 64)
                                                rr, rk = fpool.get()
                                                if mi == 1:
                                                    act(rr[dr, :FN], ps[dr, AB, :FN], AF.Ln, [("ps", AB), "esink"], [rk],
                                                        bias=esink[dr, h:h + 1])
                                                else:
                                                    act(rr[dr, :FN], ps[dr, AB, :FN], AF.Ln, [("ps", AB)], [rk])
                                                act(rr[pr_, :FN], rr[dr, :FN], AF.Exp, [rk], [rk], scale=-1.0)
                                                tt(rr[pr_, :FN], rr[pr_, :FN], Zb[pr_, c, gs], ALU.mult, [rk, (zn, c, jq)], [rk])
                                                tt(hT[pr_, mi * 4 + c, gs], ps[pr_, AB, :FN], rr[pr_, :FN], ALU.mult,
                                                   [("ps", AB), rk], [hk(mi * 4 + c, jq)])
                                            fins = [(0, fin)]
                                        stream.append(dict(w=len(subs) * sn, mm=mm_, post=post, pv=pv, fins=fins))
                run_stream(stream, 1, extra=next_mod)

            if DBG.get("stop") == "attn":
                return
            PHASE[0] = f"{grp}{l}:outproj"
            s0_ = w_next()
            s1_ = w_next(issue=False)
            SB2 = 6
            for (oc0, ON) in ([(0, 256)] if last else [(j_ * 512, 512) for j_ in range(NT)]):
                j = oc0 // 512
                cs = slice(oc0, oc0 + ON)
                prev_sq = None
                for fo in range(8):
                    slot = s0_ if fo < 4 else s1_
                    b = ring()
                    for kc in range(8):
                        mm(ps[:, b, :ON], wring[:, slot, kc, (fo % 4) * 128:(fo % 4 + 1) * 128], hT[:, kc, cs],
                           kc == 0, kc == 7, R=[("w", slot), hk(kc, j)], W=[("ps", b)])
                    if prev_sq is not None:
                        mm(ps[:, SB2, :ON], ones16[:], prev_sq[0][:, :ON], prev_sq[2] == 0, False, R=[prev_sq[1], "ones16"],
                           W=[("ps", SB2)])
                    act(ybuf[:, fo, :ON], ps[:, b, :ON], AF.Copy, [("ps", b)], [("ycv", fo)])
                    sq, sk = bpool.get()
                    act(sq[:, :ON], ps[:, b, :ON], AF.Square, [("ps", b)], [sk])
                    prev_sq = (sq, sk, fo)
                mm(ps[:, SB2, :ON], ones16[:], prev_sq[0][:, :ON], False, True, R=[prev_sq[1], "ones16"], W=[("ps", SB2)])
                rs, rk = rpool.get()
                rstd_from(ps[:, SB2, :ON], 1.0 / 1024.0, [("ps", SB2), "epsb"], rs[:, :ON], rk, n=ON)
                for c in range(8):
                    tmp, tk = fpool.get()
                    tt(tmp[:, :ON], ybuf[:, c, :ON], rs[:, :ON], ALU.mult, [("ycv", c), rk], [tk])
                    stt(xT[:, c, cs], tmp[:, :ON], G1[:, l, c, ck:ck + 1], xT[:, c, cs], ALU.mult, ALU.add,
                        [tk, xk(c, j)] + modR, [xk(c, j)])

        if (not isS) and hoist:
            PHASE[0] = "S:prologue2"
            prologue_blocks(xs_d, range(4, 8))
        PHASE[0] = f"{grp}:epilogue"
        own_only = isS and DBG["layers"] == 4
        for blk in range(2 if own_only else min(T // 128, 2 if isS else 99)):
            st = blk % 2
            for half in range(2):
                b = ring()
                for q in range(4):
                    fc = half * 4 + q
                    tr(ps[:, b, q * 128:(q + 1) * 128], xT[:, fc, blk * 128:(blk + 1) * 128],
                       R=[xk(fc, blk // 4), "consts"], W=[("ps", b)])
                evac_copy(stage[:, st, half * 512:(half + 1) * 512], ps[:, b, :], [("ps", b)], [("stage", st, half)])
            dma("sp", y_d[blk * 128:(blk + 1) * 128, :], stage[:, st, :], R=[("stage", st, 0), ("stage", st, 1)], W=[])

    for g_ in DBG["passes"]:
        run_pass(g_)
    S.finalize()
    LAST["S"] = S

    with nc.Block() as block:
        @block.sync
        def _(e):
            S.emit("sp", e, sems, final_wait=True)

        @block.gpsimd
        def _(e):
            S.emit("pool", e, sems, final_wait=True)

        @block.tensor
        def _(e):
            S.emit("pe", e, sems)

        @block.scalar
        def _(e):
            S.emit("act", e, sems)

        @block.vector
        def _(e):
            S.emit("dve", e, sems)

    es.close()
    return nc


def _fm(v, nch):
    return np.ascontiguousarray(np.asarray(v, np.float32).reshape(nch, 128).T)


def _wgroups(W, cols):
    Wc = np.asarray(W, np.float32)[:, cols]
    G = Wc.shape[1] // 512
    return np.ascontiguousarray(Wc.reshape(8, 128, G, 512).transpose(2, 1, 0, 3))


def _even_cols():
    ar = np.arange
    cols = []
    for c in range(4):
        cols += [ar(512 + c * 128, 512 + (c + 1) * 128), ar(c * 128, (c + 1) * 128)]
    cols += [ar(1024, 1536), ar(3072, 3584), ar(1536, 2048), ar(2048, 2560), ar(2560, 3072)]
    return np.concatenate(cols)


def _odd_cols():
    ar = np.arange
    cq, cz, dq, dz = 0, 768, 1280, 2048
    cols = []
    for q0, z0 in ((cq, cz), (dq, dz)):
        for c in range(4):
            cols += [ar(q0 + c * 128, q0 + (c + 1) * 128), ar(z0 + c * 128, z0 + (c + 1) * 128)]
    ck, dk = 512, 1792
    cols += [ar(ck, ck + 64), ar(ck, ck + 64), ar(ck + 64, ck + 128), ar(ck + 64, ck + 128)]
    cols += [ar(dk, dk + 64), ar(dk, dk + 64), ar(dk + 64, dk + 128), ar(dk + 64, dk + 128)]
    cols += [ar(640, 768), ar(1920, 2048), ar(512, 640), ar(1792, 1920)]
    return np.concatenate(cols)


def _consts():
    c = np.zeros((128, NCONST), np.float32)
    cm = np.zeros((128, 4, 128), np.float32)
    p = np.arange(128)
    c[:, C_ID:C_ID + 128] = np.eye(128, dtype=np.float32)
    d = p % 64
    partner = np.where((d // 16) % 2 == 0, p + 16, p - 16)
    R = np.zeros((128, 128), np.float32)
    R[partner, p] = 1.0
    cm[:, 0] = R
    cm[:, 1] = (p[:, None] // 64 == p[None, :] // 64).astype(np.float32) / 64.0
    cm[:, 2] = (p[:, None] <= p[None, :]).astype(np.float32)
    cm[:, 3] = (p[None, :] <= p[:, None]).astype(np.float32)
    t = np.arange(DEC_SEQ)
    row = (t // 64).astype(np.float32)
    col = (t % 64).astype(np.float32)
    nf = 16
    inv = (np.float32(10000.0) ** (-np.arange(nf, dtype=np.float32) / np.float32(nf))).astype(np.float32)
    a = (d // 32)
    j = d % 16
    b = (d // 16) % 2
    pos = np.where(a[:, None] == 0, row[None, :], col[None, :]).astype(np.float32)
    ang = (pos * inv[j][:, None]).astype(np.float32)
    c[:, C_COS:C_COS + 1024] = np.cos(ang)
    sgn = np.where(b == 0, -1.0, 1.0).astype(np.float32)
    c[:, C_SIN:C_SIN + 1024] = np.sin(ang) * sgn[:, None]
    return c, cm


def _dmask(r):
    k = np.arange(128)[:, None, None]
    kb = np.arange(8)[None, :, None]
    q = np.arange(256)[None, None, :]
    tq = r * 256 + q
    tk = kb * 128 + k
    return np.ascontiguousarray((np.abs(tq - tk) <= 128).astype(np.float32).reshape(128, 2048))


_NC_CACHE = {}


def kernel(x_prompt, x_sample, cache_b_k, cache_b_v, cache_c_k, cache_c_v, cache_d_k, cache_d_v,
           c, c_ctx, norm_pre, norm_post, w_mod, b_mod, w_in_even, a_conv_w, a_conv_b, a_ln_g,
           a_ln_b, b_lambda, b_subln_g, w_out_even, w_in_odd, c_q_norm, c_k_norm, d_sink, w_out_odd):
    f = lambda a: np.asarray(a, np.float32)
    x_prompt, x_sample = f(x_prompt), f(x_sample)
    ec, oc = _even_cols(), _odd_cols()
    wl = []
    for l in range(4):
        i = l // 2
        gm = _wgroups(w_mod[l], np.arange(3072))
        if l % 2 == 0:
            gi = _wgroups(w_in_even[i], ec)
            go = _wgroups(w_out_even[i], np.arange(1024))
        else:
            gi = _wgroups(w_in_odd[i], oc)
            go = _wgroups(w_out_odd[i], np.arange(1024))
        wl.append(np.ascontiguousarray(np.concatenate([gm, gi, go], axis=0)))
    params = np.zeros((128, 4 * NPL), np.float32)
    for l in range(4):
        i = l // 2
        o = l * NPL
        params[:, o + P_NPRE:o + P_NPRE + 8] = _fm(norm_pre[l], 8)
        params[:, o + P_NPOST:o + P_NPOST + 8] = _fm(norm_post[l], 8)
        params[:, o + P_BMOD:o + P_BMOD + 24] = _fm(b_mod[l], 24)
        if l % 2 == 0:
            cw = f(a_conv_w[i])
            params[:, o + P_CONVW:o + P_CONVW + 124] = cw.reshape(31, 4, 128).transpose(2, 1, 0).reshape(128, 124)
            params[:, o + P_CONVB:o + P_CONVB + 4] = _fm(a_conv_b[i], 4)
            params[:, o + P_LNG:o + P_LNG + 4] = _fm(a_ln_g[i], 4)
            params[:, o + P_LNB:o + P_LNB + 4] = _fm(a_ln_b[i], 4)
            params[:, o + P_SUBG] = f(b_subln_g[i])
            params[:, o + P_LAM:o + P_LAM + 256] = f(b_lambda[i]).reshape(1, 256)
        else:
            params[:, o + P_QN] = np.tile(f(c_q_norm[i]), 2)
            params[:, o + P_KN] = np.tile(f(c_k_norm[i]), 2)
            params[:, o + P_SINK:o + P_SINK + 8] = f(d_sink[i]).reshape(1, 8)
            params[:, o + P_KNF:o + P_KNF + 64] = f(c_k_norm[i]).reshape(1, 64)
    consts, cmat = _consts()
    cbk = f(cache_b_k).reshape(2, 2, 512, 512)
    cbv = f(cache_b_v).reshape(2, 2, 512, 512)
    cck = f(cache_c_k).reshape(2, 2, 512, 128)
    ccv = f(cache_c_v).reshape(2, 2, 512, 128)
    cdk = f(cache_d_k).reshape(2, 2, 512, 128)
    cdv = f(cache_d_v).reshape(2, 2, 512, 128)
    c, c_ctx = f(c), f(c_ctx)

    in_maps = []
    for core in range(N_CORES):
        s = core // 4
        cond = np.zeros((128, 16), np.float32)
        cond[:, 0::2] = _fm(c_ctx, 8)
        cond[:, 1::2] = _fm(c[s], 8)
        m = {
            "xp": np.ascontiguousarray(x_prompt[2 * core:2 * core + 2].reshape(512, 1024)),
            "xs": np.ascontiguousarray(x_sample[s]),
            "cond": cond, "params": params, "consts": consts, "cmat": cmat,
            "sel": np.ascontiguousarray(np.tile(np.eye(4, dtype=np.float32)[core % 4][None, :], (128, 1))),
            "dmask": _dmask(core % 4),
            "ropeown": np.ascontiguousarray(np.concatenate(
                [consts[:, C_COS + (core % 4) * 256:C_COS + (core % 4 + 1) * 256],
                 consts[:, C_SIN + (core % 4) * 256:C_SIN + (core % 4 + 1) * 256]], axis=1)),
            "cbk": np.ascontiguousarray(cbk[s]), "cbv": np.ascontiguousarray(cbv[s]),
            "cck": np.ascontiguousarray(cck[s]), "ccv": np.ascontiguousarray(ccv[s]),
            "cdk": np.ascontiguousarray(cdk[s]), "cdv": np.ascontiguousarray(cdv[s]),
        }
        for l in range(DBG["layers"]):
            m[f"wl{l}"] = wl[l]
        in_maps.append(m)

    if "nc" not in _NC_CACHE:
        _NC_CACHE["nc"] = build_program()
    nc = _NC_CACHE["nc"]
    res = run_bass_kernel_spmd(nc, in_maps, core_ids=list(range(N_CORES)))
    R = res.results
    y_prompt = np.concatenate([R[i]["yp"].reshape(2, 256, 1024) for i in range(8)], axis=0)
    y_sample = np.stack([np.concatenate([R[4 * s_ + r_]["ys"] for r_ in range(4)], axis=0) for s_ in range(2)], axis=0)
    cat = lambda k: np.concatenate([R[i][k] for i in range(8)], axis=0)
    new_b_k = cat("nbk").reshape(16, 2, 256, 4, 2, 64)
    new_b_v = cat("nbv").reshape(16, 2, 256, 4, 128)
    new_c_k = cat("nck").reshape(16, 2, 256, 2, 64)
    new_c_v = cat("ncv").reshape(16, 2, 256, 2, 64)
    new_d_k = cat("ndk").reshape(16, 2, 256, 2, 64)
    new_d_v = cat("ndv").reshape(16, 2, 256, 2, 64)
    return tuple(np.asarray(a, np.float32) for a in
                 (y_prompt, y_sample, new_b_k, new_b_v, new_c_k, new_c_v, new_d_k, new_d_v))
```
